# Optimizing a Trainium2 kernel written in Bass

```python
import math
import jax, jax.numpy as jnp
from jax import lax
import numpy as np

D_MODEL = 2048
BATCH = 4
SEQ = 4096
DEPTH = 1

N_ATTN_HEADS = 8
ATTN_QK_DIM = 64
ATTN_V_DIM = 2 * ATTN_QK_DIM
ATTN_QK_WIDTH = N_ATTN_HEADS * 2 * ATTN_QK_DIM
ATTN_WIDTH = N_ATTN_HEADS * ATTN_V_DIM
Q_BLOCK = 128
SG_WIDTH = D_MODEL // 2
N_SG_GROUPS = 8
SG_GROUP_DIM = SG_WIDTH // N_SG_GROUPS
SG_CHUNK = 128
N_BRANCHES = 2
D_FF = 4 * D_MODEL
Q_OFF = 0
K_OFF = Q_OFF + ATTN_QK_WIDTH
V_OFF = K_OFF + ATTN_QK_WIDTH
U_OFF = V_OFF + ATTN_WIDTH
SGV_OFF = U_OFF + SG_WIDTH
GATE_OFF = SGV_OFF + SG_WIDTH
IN_WIDTH = GATE_OFF + N_BRANCHES * D_MODEL
DEEPNORM_ALPHA = (2.0 * DEPTH) ** 0.25
DEEPNORM_BETA = (8.0 * DEPTH) ** -0.25
LN_EPS = 1e-5

kernel_name = "hybrid_diffattn_sgu_gated_deepnorm_block"


def layer_norm(x, g, b):
    xf = x.astype(jnp.float32)
    mu = jnp.mean(xf, axis=-1, keepdims=True)
    var = jnp.mean(jnp.square(xf - mu), axis=-1, keepdims=True)
    y = (xf - mu) * lax.rsqrt(var + LN_EPS) * g.astype(jnp.float32) + b.astype(jnp.float32)
    return y.astype(x.dtype)


def alibi_slopes(n_heads):
    i = jnp.arange(1, n_heads + 1, dtype=jnp.float32)
    return jnp.exp2(-8.0 * i / n_heads)


def diff_attention(q, k, v, lam, lambda_init, sub_gain):
    B, S, H, _, dk = q.shape
    nb = S // Q_BLOCK
    slopes = alibi_slopes(H)
    k_pos = jnp.arange(S, dtype=jnp.float32)
    scale = dk ** -0.5
    q_blocks = q.reshape(B, nb, Q_BLOCK, H, 2, dk).transpose(1, 0, 2, 3, 4, 5)

    def one_block(args):
        q_blk, i = args
        q_pos = (i * Q_BLOCK + jnp.arange(Q_BLOCK)).astype(jnp.float32)
        dist = jnp.abs(q_pos[:, None] - k_pos[None, :])
        bias = -slopes[:, None, None] * dist[None]
        s = jnp.einsum('bqhcd,bkhcd->bhcqk', q_blk, k,
                       preferred_element_type=jnp.float32) * scale
        p = jax.nn.softmax(s + bias[None, :, None], axis=-1)
        w = p[:, :, 0] - lam * p[:, :, 1]
        return jnp.einsum('bhqk,bkhd->bqhd', w.astype(v.dtype), v)

    o = lax.map(one_block, (q_blocks, jnp.arange(nb)))
    o = o.transpose(1, 0, 2, 3, 4).reshape(B, S, H, v.shape[-1])
    of = o.astype(jnp.float32)
    of = of * lax.rsqrt(jnp.mean(jnp.square(of), axis=-1, keepdims=True) + LN_EPS)
    of = of * sub_gain.astype(jnp.float32) * (1.0 - lambda_init)
    return of.astype(v.dtype).reshape(B, S, H * v.shape[-1])


def spatial_gating(u, v, ln_g, ln_b, w_s, b_s):
    B, S, _ = v.shape
    v = layer_norm(v, ln_g, ln_b)
    vc = v.reshape(B, S // SG_CHUNK, SG_CHUNK, N_SG_GROUPS, SG_GROUP_DIM)
    mixed = jnp.einsum('gts,bcsgd->bctgd', w_s, vc) + b_s.T[:, :, None]
    return u * mixed.reshape(B, S, SG_WIDTH)


def setup_inputs(seed: int = 0) -> dict:
    key = jax.random.key(seed)
    ks = jax.random.split(key, 24)
    f32 = jnp.float32
    L, D = DEPTH, D_MODEL

    def nrm(k, shape, scale):
        return jax.random.normal(k, shape, f32) * scale

    return {
        "x": jax.random.normal(ks[0], (BATCH, SEQ, D), f32),
        "w_in": nrm(ks[1], (L, D, IN_WIDTH), D ** -0.5),
        "lambda_q1": nrm(ks[2], (L, ATTN_QK_DIM), 0.1),
        "lambda_k1": nrm(ks[3], (L, ATTN_QK_DIM), 0.1),
        "lambda_q2": nrm(ks[4], (L, ATTN_QK_DIM), 0.1),
        "lambda_k2": nrm(ks[5], (L, ATTN_QK_DIM), 0.1),
        "attn_subln_g": 1.0 + nrm(ks[6], (L, ATTN_V_DIM), 0.02),
        "sg_ln_g": 1.0 + nrm(ks[7], (L, SG_WIDTH), 0.02),
        "sg_ln_b": nrm(ks[8], (L, SG_WIDTH), 0.02),
        "sg_w_s": nrm(ks[9], (L, N_SG_GROUPS, SG_CHUNK, SG_CHUNK), SG_CHUNK ** -0.5),
        "sg_b_s": 1.0 + nrm(ks[10], (L, N_SG_GROUPS, SG_CHUNK), 0.02),
        "b_gate": nrm(ks[11], (L, N_BRANCHES * D), 0.02),
        "w_branch_attn": nrm(ks[12], (L, ATTN_WIDTH, D), ATTN_WIDTH ** -0.5),
        "w_branch_sg": nrm(ks[13], (L, SG_WIDTH, D), SG_WIDTH ** -0.5),
        "w_o": nrm(ks[14], (L, D, D), D ** -0.5 * DEEPNORM_BETA),
        "ln1_g": 1.0 + nrm(ks[15], (L, D), 0.02),
        "ln1_b": nrm(ks[16], (L, D), 0.02),
        "w_up": nrm(ks[17], (L, D, D_FF), D ** -0.5),
        "w_down": nrm(ks[18], (L, D_FF, D), D_FF ** -0.5 * DEEPNORM_BETA),
        "ln2_g": 1.0 + nrm(ks[19], (L, D), 0.02),
        "ln2_b": nrm(ks[20], (L, D), 0.02),
    }


def reference(x, w_in, lambda_q1, lambda_k1, lambda_q2, lambda_k2, attn_subln_g,
              sg_ln_g, sg_ln_b, sg_w_s, sg_b_s, b_gate, w_branch_attn, w_branch_sg,
              w_o, ln1_g, ln1_b, w_up, w_down, ln2_g, ln2_b):
    B, S, D = x.shape
    f32 = jnp.float32
    for l in range(DEPTH):
        lambda_init = 0.8 - 0.6 * math.exp(-0.3 * l)
        lam = (jnp.exp(jnp.sum(lambda_q1[l].astype(f32) * lambda_k1[l].astype(f32)))
               - jnp.exp(jnp.sum(lambda_q2[l].astype(f32) * lambda_k2[l].astype(f32)))
               + lambda_init)

        proj = x @ w_in[l]
        q = proj[..., Q_OFF:K_OFF].reshape(B, S, N_ATTN_HEADS, 2, ATTN_QK_DIM)
        k = proj[..., K_OFF:V_OFF].reshape(B, S, N_ATTN_HEADS, 2, ATTN_QK_DIM)
        v = proj[..., V_OFF:U_OFF].reshape(B, S, N_ATTN_HEADS, ATTN_V_DIM)
        sg = jax.nn.gelu(proj[..., U_OFF:GATE_OFF], approximate=False)
        sg_u, sg_v = sg[..., :SG_WIDTH], sg[..., SG_WIDTH:]
        gates = jax.nn.sigmoid(proj[..., GATE_OFF:] + b_gate[l]).reshape(B, S, N_BRANCHES, D)

        attn_out = diff_attention(q, k, v, lam, lambda_init, attn_subln_g[l])
        sg_out = spatial_gating(sg_u, sg_v, sg_ln_g[l], sg_ln_b[l], sg_w_s[l], sg_b_s[l])

        merged = (gates[:, :, 0] * (attn_out @ w_branch_attn[l])
                  + gates[:, :, 1] * (sg_out @ w_branch_sg[l]))
        x = layer_norm(DEEPNORM_ALPHA * x + merged @ w_o[l], ln1_g[l], ln1_b[l])

        h = jnp.square(jax.nn.relu(x @ w_up[l]))
        x = layer_norm(DEEPNORM_ALPHA * x + h @ w_down[l], ln2_g[l], ln2_b[l])
    return x
```

```python
import math
from contextlib import ExitStack

import numpy as np
import concourse.bass as bass
import concourse.mybir as mybir
from concourse.bass_utils import run_bass_kernel_spmd

F32 = mybir.dt.float32
BF16 = mybir.dt.bfloat16
AF = mybir.ActivationFunctionType
ALU = mybir.AluOpType
AX = mybir.AxisListType

D = 2048
SEQ = 4096
TOWN = 2048
NH = 8
Q_OFF, K_OFF, V_OFF, U_OFF, SGV_OFF, GATE_OFF = 0, 1024, 2048, 3072, 4096, 5120
IN_W = 9216
DFF = 8192
ALPHA = 2.0 ** 0.25
LN_EPS = 1e-5
LAMBDA_INIT = 0.8 - 0.6 * math.exp(0.0)
SLOPES = [2.0 ** (-(h + 1)) for h in range(NH)]
SCALE = 0.125

T_KBFAR, T_KBLO, T_KBHI, T_QFAR, T_QLO, T_QHI, T_TW, T_END = 0, 128, 224, 320, 448, 480, 512, 1408


class Op:
    __slots__ = ("eng", "fn", "signal", "sigval", "deps", "sem", "is_dma", "idx")


class Prog:
    ENGS = ("pe", "act", "dve", "pool", "sp")

    def __init__(self, nc, es):
        self.nc = nc
        self.es = es
        self.ops = {e: [] for e in self.ENGS}
        self.last_w = {}
        self.readers = {}
        self.esem = {e: es.enter_context(nc.semaphore("sem_" + e)) for e in self.ENGS}
        self.dma_sems = {}
        self.dma_cnt = {}

    def dsem(self, name):
        if name not in self.dma_sems:
            self.dma_sems[name] = self.es.enter_context(self.nc.semaphore("dq_" + name))
            self.dma_cnt[name] = 0
        return name

    def add(self, eng, fn, reads=(), writes=(), dma=None, after=()):
        op = Op()
        op.eng, op.fn, op.signal, op.sigval, op.deps = eng, fn, False, 0, {}
        op.is_dma = dma is not None
        op.sem = None
        if op.is_dma:
            self.dsem(dma)
            op.sem = dma
            self.dma_cnt[dma] += 16
            op.sigval = self.dma_cnt[dma]
            op.signal = True

        def need(prev, raw):
            if prev is None or prev is op:
                return
            if not (prev.is_dma or op.is_dma):
                if prev.eng == eng:
                    if eng == "pe":
                        return
            prev.signal = True
            op.deps[id(prev)] = prev

        for a_ in after:
            need(a_, True)
        for r in reads:
            need(self.last_w.get(r), True)
        for w in writes:
            need(self.last_w.get(w), False)
            for rd in self.readers.get(w, {}).values():
                need(rd, False)
        skey = ("d", op.sem) if op.is_dma else ("e", eng)
        for r in reads:
            self.readers.setdefault(r, {})[skey] = op
        for w in writes:
            self.last_w[w] = op
            self.readers[w] = {}
        self.ops[eng].append(op)
        return op

    def barrier(self):
        lasts = []
        for e in self.ENGS:
            for op in reversed(self.ops[e]):
                if not op.is_dma:
                    lasts.append(op)
                    break
        seen = {}
        for e in self.ENGS:
            for op in self.ops[e]:
                if op.is_dma:
                    seen[op.sem] = op
        lasts += list(seen.values())
        for e in self.ENGS:
            op = Op()
            op.eng, op.fn, op.signal, op.sigval, op.deps = e, (lambda h: h.nop()), False, 0, {}
            op.is_dma, op.sem = False, None
            for l in lasts:
                if (not l.is_dma) and l.eng == e:
                    continue
                l.signal = True
                op.deps[id(l)] = l
            self.ops[e].append(op)

    def emit(self):
        nc = self.nc
        for e in self.ENGS:
            cnt = 0
            for op in self.ops[e]:
                if op.is_dma:
                    continue
                if op.signal:
                    cnt += 1
                    op.sigval = cnt

        def semof(op):
            if op.is_dma:
                return ("d", op.sem), self.dma_sems[op.sem]
            return ("e", op.eng), self.esem[op.eng]

        def run(e, handle):
            waited = {}
            for op in self.ops[e]:
                need = {}
                for d in op.deps.values():
                    k, s = semof(d)
                    if need.get(k, (None, 0))[1] < d.sigval:
                        need[k] = (s, d.sigval)
                for k, (s, v) in need.items():
                    if waited.get(k, 0) < v:
                        handle.wait_ge(s, v)
                        waited[k] = v
                ins = op.fn(handle)
                if op.signal:
                    if op.is_dma:
                        ins.then_inc(self.dma_sems[op.sem], 16)
                    else:
                        ins.then_inc(self.esem[e], 1)

        with nc.Block() as block:
            @block.tensor
            def _(h):
                run("pe", h)

            @block.scalar
            def _(h):
                run("act", h)

            @block.vector
            def _(h):
                run("dve", h)

            @block.gpsimd
            def _(h):
                run("pool", h)

            @block.sync
            def _(h):
                run("sp", h)


def build_nc(debug=None):
    nc = bass.Bass("TRN2", target_bir_lowering=False)
    dt_in = lambda name, shape: nc.dram_tensor(name, shape, F32, kind="ExternalInput").ap()
    x_perm = dt_in("x_perm", [SEQ, D])
    w_in = dt_in("w_in", [D, IN_W])
    lam4 = dt_in("lam4", [4, 64])
    subln_g = dt_in("subln_g", [1, 128])
    sg_ln_g = dt_in("sg_ln_g", [1, 1024])
    sg_ln_b = dt_in("sg_ln_b", [1, 1024])
    sg_w_s = dt_in("sg_w_s", [128, 8, 128])
    sg_b_s = dt_in("sg_b_s", [1, 1024])
    b_gate = dt_in("b_gate", [128, 32])
    w_ba = dt_in("w_ba", [1024, D])
    w_bs = dt_in("w_bs", [1024, D])
    w_o = dt_in("w_o", [D, D])
    ln1_g = dt_in("ln1_g", [1, D])
    ln1_b = dt_in("ln1_b", [1, D])
    w_up = dt_in("w_up", [D, DFF])
    w_down = dt_in("w_down", [DFF, D])
    ln2_g = dt_in("ln2_g", [1, D])
    ln2_b = dt_in("ln2_b", [1, D])
    tabs_d = dt_in("tabs", [128, T_END])
    ident_d = dt_in("ident", [128, 128])
    out = nc.dram_tensor("out", [TOWN, D], F32, kind="ExternalOutput").ap()
    dbg = None
    if debug is not None:
        dbg = nc.dram_tensor("dbg", [128, NH * TOWN], BF16, kind="ExternalOutput").ap()

    xTs = nc.dram_tensor("xTs", [8, 128, 8192], BF16).ap()
    scr_up = nc.dram_tensor("scr_up", [16, 128, 8192], BF16).ap()
    scr_dn = nc.dram_tensor("scr_dn", [16, 128, 8192], BF16).ap()
    x_v = x_perm.rearrange("(b p) d -> p b d", p=128)
    w_in_v = w_in.rearrange("(c p) n -> p c n", p=128)
    w_ba_v = w_ba.rearrange("(c p) n -> p c n", p=128)
    w_bs_v = w_bs.rearrange("(c p) n -> p c n", p=128)
    w_o_v = w_o.rearrange("(c p) n -> p c n", p=128)
    w_up_v = w_up.rearrange("(c p) n -> p c n", p=128)
    w_dn_v = w_down.rearrange("(c p) n -> p c n", p=128)
    out_v = out.rearrange("(b p) d -> p b d", p=128)

    with ExitStack() as es:
        P = Prog(nc, es)

        def sb(name, shape, dt):
            return es.enter_context(nc.sbuf_tensor("sb_" + name, shape, dt))

        ps = [es.enter_context(nc.psum_tensor("ps%d" % i, [128, 512], F32)) for i in range(8)]
        psTb = {6: ps[6][:].bitcast(BF16), 7: ps[7][:].bitcast(BF16)}

        ident = sb("ident", [128, 128], BF16)
        tabs = sb("tabs", [128, T_END], F32)
        qf = sb("qf", [128, 192], F32)
        lamt = sb("lamt", [128, 4, 64], F32)
        lprod = sb("lprod", [128, 2, 64], F32)
        lsum = sb("lsum", [128, 2], F32)
        lexp = sb("lexp", [128, 2], F32)
        neglam = sb("neglam", [128, 1], F32)
        gainT = sb("gainT", [128, 128], F32)
        attnT = sb("attnT", [128, NH, TOWN], BF16)
        ARENA = 80 * 1024
        arena = sb("arena", [128, ARENA], BF16)

        class Carver:
            def __init__(self):
                self.off = 0

            def take(self, shape, dt):
                n = int(np.prod(shape))
                units = n * (2 if dt == F32 else 1)
                units = (units + 15) // 16 * 16
                a = arena[:, self.off:self.off + units]
                self.off += units
                assert self.off <= ARENA, self.off
                if dt == F32:
                    a = a.bitcast(F32)
                a = a[:, 0:n]
                if len(shape) == 2:
                    return a.rearrange("p (a b) -> p a b", a=shape[0])
                if len(shape) == 3:
                    return a.rearrange("p (a b c) -> p a b c", a=shape[0], b=shape[1])
                return a

        P.add("pool", lambda e: e.dma_start(out=ident[:], in_=ident_d), writes=["ident"], dma="c_ident")
        P.add("sp", lambda e: e.dma_start(out=tabs[:], in_=tabs_d), writes=["tabs"], dma="c_tabs")
        P.add("sp", lambda e: e.dma_start(out=lamt[:].rearrange("p a b -> p (a b)"),
                                          in_=lam4.rearrange("a b -> (a b)").partition_broadcast(128)),
              writes=["lamt"], dma="c_lam")
        P.add("sp", lambda e: e.dma_start(out=gainT[:], in_=subln_g[0, :].partition_broadcast(128)),
              writes=["gainT"], dma="c_gain")
        P.add("act", lambda e: e.activation(out=qf[:], in_=tabs[:, T_QFAR:T_TW], func=AF.Exp),
              reads=["tabs"], writes=["qf"])
        P.add("dve", lambda e: e.tensor_tensor(out=lprod[:], in0=lamt[:, 0:4:2, :], in1=lamt[:, 1:4:2, :], op=ALU.mult),
              reads=["lamt"], writes=["lprod"])
        P.add("dve", lambda e: e.tensor_reduce(out=lsum[:], in_=lprod[:], axis=AX.X, op=ALU.add),
              reads=["lprod"], writes=["lsum"])
        P.add("act", lambda e: e.activation(out=lexp[:], in_=lsum[:], func=AF.Exp), reads=["lsum"], writes=["lexp"])
        P.add("dve", lambda e: e.tensor_tensor(out=neglam[:], in0=lexp[:, 1:2], in1=lexp[:, 0:1], op=ALU.subtract),
              reads=["lexp"], writes=["neglam0"])
        P.add("dve", lambda e: e.tensor_scalar(out=neglam[:], in0=neglam[:], scalar1=-LAMBDA_INIT, scalar2=None, op0=ALU.add),
              reads=["neglam0"], writes=["neglam"])
        P.add("dve", lambda e: e.tensor_scalar(out=gainT[:], in0=gainT[:], scalar1=1.0 - LAMBDA_INIT, scalar2=None, op0=ALU.mult),
              reads=["gainT"], writes=["gainT"])

        def early(reads, src_bf16=None):
            if src_bf16 is not None:
                P.add("sp", lambda e: e.dma_start(out=dbg[:, 0:src_bf16.shape[1]], in_=src_bf16), reads=reads,
                      writes=["dbgout"], dma="o_dbg")
            P.add("sp", lambda e: e.dma_start(out=out_v[:, 0, 0:128], in_=gainT[:]), reads=["gainT", "neglam", "qf"],
                  writes=["outx"], dma="o_fin")
            P.add("sp", lambda e: e.nop(), reads=["outx", "dbgout"], writes=["fin"])
            P.emit()
            return nc

        if debug == "setup":
            return early([])

        cv = Carver()
        Wg = [cv.take([16, 768], BF16) for _ in range(1)]
        xs = [cv.take([4, 2048], BF16) for _ in range(2)]
        xTb = [cv.take([16, 512], BF16) for _ in range(2)]
        QT = cv.take([2, TOWN], BF16)
        KT = cv.take([2, SEQ], BF16)
        Vt = cv.take([32, 2, 130], BF16)
        Et = [[cv.take([512], BF16) for _ in range(2)] for _ in range(2)]
        Oacc2 = [cv.take([8, 130], F32) for _ in range(2)]
        sdiag = [cv.take([512], F32) for _ in range(2)]
        rz = cv.take([4, 2], F32)
        rz1l = cv.take([4], F32)
        tA = cv.take([4, 128], F32)
        tB = cv.take([4, 128], F32)
        ssum = cv.take([4], F32)
        rstd = cv.take([4], F32)
        ofb = cv.take([4, 128], BF16)

        P.add("dve", lambda e: e.memset(Vt[:, :, :, 128:130], 1.0), writes=["Vones"])

        rot = {"ps": 0, "psT": 0, "xs": 0, "E": 0}

        def load_x_tile(ti):
            slot = rot["xs"] % 2
            rot["xs"] += 1
            P.add("pool", lambda e: e.dma_start(out=xs[slot], in_=x_v[:, 4 * ti:4 * ti + 4, :]),
                  writes=["xs%d" % slot], dma="xs%d" % slot)
            return slot

        def transpose_tile(src, src_key, dst, dst_key_fmt, evac="dve"):
            for c in range(16):
                sl = 6 + rot["psT"] % 2
                rot["psT"] += 1
                pv = psTb[sl][:, 0:512]
                for b in range(4):
                    P.add("pe", lambda e, b=b, c=c, pv=pv: e.transpose(
                        out=pv[:, b * 128:(b + 1) * 128], in_=src[:, b, c * 128:(c + 1) * 128], identity=ident[:]),
                        reads=[src_key, "ident"], writes=["ps%d" % sl])
                if evac == "dve":
                    P.add("dve", lambda e, c=c, pv=pv: e.tensor_copy(out=dst[:, c, :], in_=pv),
                          reads=["ps%d" % sl], writes=[dst_key_fmt % c])
                else:
                    P.add("act", lambda e, c=c, pv=pv: e.activation(out=dst[:, c, :], in_=pv, func=AF.Copy),
                          reads=["ps%d" % sl], writes=[dst_key_fmt % c])

        def nbank():
            b = rot["ps"] % 4
            rot["ps"] += 1
            return b

        cast_jobs = []
        for k in range(16):
            cast_jobs.append(("scr_up%d" % k, scr_up[k].rearrange("p (c n) -> p c n", c=16),
                              w_up_v[:, :, k * 512:(k + 1) * 512]))
        for kd in range(16):
            fhp, ct_ = kd // 4, kd % 4
            r0_ = (fhp // 2) * 32 + (fhp % 2) * 16
            cast_jobs.append(("scr_dn%d" % kd, scr_dn[kd].rearrange("p (c n) -> p c n", c=16),
                              w_dn_v[:, r0_:r0_ + 16, ct_ * 512:(ct_ + 1) * 512]))
        cast_plan = {0: 0, 1: 10, 2: 11, 3: 11}

        def emit_cast(pace_op):
            if not cast_jobs:
                return
            key, dst, src = cast_jobs.pop(0)
            P.add("pool", lambda e, dst=dst, src=src: e.dma_start(out=dst, in_=src),
                  writes=[key], dma="cast", after=([pace_op] if pace_op is not None else []))

        for g in range(4):
            wsl = 0
            wt = Wg[wsl]
            for i, off in enumerate((Q_OFF, K_OFF, V_OFF)):
                P.add("pool", lambda e, i=i, off=off, wt=wt, g=g: e.dma_start(
                    out=wt[:, :, i * 256:(i + 1) * 256], in_=w_in_v[:, :, off + 256 * g: off + 256 * g + 256]),
                    writes=["Wg%d_%d" % (wsl, i)], dma="Wg%d_%d" % (wsl, i))
            xslots = {}
            if g == 0:
                xslots[0] = load_x_tile(0)
            for ti in range(8):
                xsl = (g * 8 + ti) % 2
                xT = xTb[xsl]
                xkf = "xT%d_" % xsl + "%d"
                if g == 0:
                    if ti + 1 < 8:
                        xslots[ti + 1] = load_x_tile(ti + 1)
                    sl = xslots[ti]
                    transpose_tile(xs[sl], "xs%d" % sl, xT, xkf)
                    P.add("sp", lambda e, ti=ti, xT=xT: e.dma_start(out=xTs[ti], in_=xT.rearrange("p a b -> p (a b)")),
                          reads=[xkf % c for c in range(16)], writes=["xTs%d" % ti], dma="xTs_w%d" % xsl)
                else:
                    P.add("sp", lambda e, ti=ti, xT=xT: e.dma_start(out=xT.rearrange("p a b -> p (a b)"), in_=xTs[ti]),
                          reads=["xTs%d" % ti], writes=[xkf % c for c in range(16)], dma="xTl%d" % xsl)
                own = ti < 4
                for kind in ((0, 1) if own else (1,)):
                    dstT = QT if kind == 0 else KT
                    for hh in range(2):
                        bk = nbank()
                        for c in range(16):
                            P.add("pe", lambda e, c=c, hh=hh, kind=kind, bk=bk, xT=xT: e.matmul(
                                out=ps[bk][:], lhsT=wt[:, c, kind * 256 + hh * 128: kind * 256 + hh * 128 + 128],
                                rhs=xT[:, c, :], start=(c == 0), stop=(c == 15)),
                                reads=[xkf % c, "Wg%d_%d" % (wsl, kind)], writes=["ps%d" % bk])
                        key = ("Q%d_%d" if kind == 0 else "K%d_%d") % (hh, ti)
                        P.add("act", lambda e, hh=hh, bk=bk, dstT=dstT, ti=ti: e.activation(
                            out=dstT[:, hh, ti * 512:(ti + 1) * 512], in_=ps[bk][:], func=AF.Copy),
                            reads=["ps%d" % bk], writes=[key])
                for b in range(4):
                    bk = nbank()
                    for c in range(16):
                        P.add("pe", lambda e, c=c, b=b, bk=bk, xT=xT: e.matmul(
                            out=ps[bk][:, 0:256], lhsT=xT[:, c, b * 128:(b + 1) * 128],
                            rhs=wt[:, c, 512:768], start=(c == 0), stop=(c == 15)),
                            reads=[xkf % c, "Wg%d_2" % wsl], writes=["ps%d" % bk])
                    P.add("act", lambda e, b=b, bk=bk, ti=ti: e.activation(
                        out=Vt[:, ti * 4 + b, :, 0:128],
                        in_=ps[bk][:, 0:256].rearrange("p (a b) -> p a b", a=2), func=AF.Copy),
                        reads=["ps%d" % bk, "Vones"], writes=["V%d" % (ti * 4 + b)])

            if debug == "inproj":
                return early(["Q0_%d" % t for t in range(4)] + ["Q1_%d" % t for t in range(4)],
                             QT.rearrange("p a b -> p (a b)"))
            SKIP_T = 125.0
            steps = []
            for hh in range(2):
                h = 2 * g + hh
                m = SLOPES[h]
                for t in range(4):
                    cats = []
                    far = [kb for kb in range(16, 32) if m * (128 * (kb - 16) + 1) <= SKIP_T]
                    cats.append(("far", far))
                    if t > 0:
                        lo = [kb for kb in range(0, 4 * t) if m * (128 * (4 * t - kb - 1) + 1) <= SKIP_T]
                        cats.append(("lo", lo))
                    cats.append(("diag", list(range(4 * t, 4 * t + 4))))
                    if t < 3:
                        hi = [kb for kb in range(4 * t + 4, 16) if m * (128 * (kb - 4 * t - 4) + 1) <= SKIP_T]
                        cats.append(("hi", hi))
                    cats = [(cn, kbs) for cn, kbs in cats if kbs]
                    nsteps = sum(len(kbs) for _, kbs in cats)
                    cnt = 0
                    for ci, (cname, kbs) in enumerate(cats):
                        for ki, kb in enumerate(kbs):
                            cnt += 1
                            steps.append(dict(hh=hh, h=h, m=m, t=t, cname=cname, kb=kb, ki=ki,
                                              last=(ki == len(kbs) - 1), first_cat=(ci == 0),
                                              final=(cnt == nsteps)))

            def emit_qk(sp_):
                esl = rot["E"] % 2
                rot["E"] += 1
                sp_["esl"] = esl
                hh, kb, t = sp_["hh"], sp_["kb"], sp_["t"]
                for c in range(2):
                    bk = c * 2 + esl
                    P.add("pe", lambda e, c=c, kb=kb, bk=bk, hh=hh, t=t: e.matmul(
                        out=ps[bk][:], lhsT=KT[c * 64:(c + 1) * 64, hh, kb * 128:(kb + 1) * 128],
                        rhs=QT[c * 64:(c + 1) * 64, hh, t * 512:(t + 1) * 512], start=True, stop=True),
                        reads=["K%d_%d" % (hh, kb // 4), "Q%d_%d" % (hh, t)], writes=["ps%d" % bk])

            def emit_exp(sp_):
                esl, hh, h, m, kb, t, cname = (sp_[k] for k in ("esl", "hh", "h", "m", "kb", "t", "cname"))
                for c in range(2):
                    bk = c * 2 + esl
                    if cname == "diag":
                        a = kb - 4 * t
                        w0 = 384 - 128 * a
                        P.add("dve", lambda e, bk=bk, w0=w0, c=c, m=m: e.scalar_tensor_tensor(
                            out=sdiag[c], in0=tabs[:, T_TW + w0:T_TW + w0 + 512], scalar=-m / SCALE,
                            in1=ps[bk][:], op0=ALU.mult, op1=ALU.add),
                            reads=["ps%d" % bk, "tabs"], writes=["sdiag%d" % c])
                        sp_["exp_op"] = P.add("act", lambda e, c=c, esl=esl: e.activation(
                            out=Et[c][esl], in_=sdiag[c], func=AF.Exp, scale=SCALE),
                            reads=["sdiag%d" % c], writes=["E%d%d" % (c, esl)])
                    else:
                        if cname == "far":
                            col = T_KBFAR + (kb - 16) * 8 + h
                        elif cname == "lo":
                            col = T_KBLO + (4 * t - kb - 1) * 8 + h
                        else:
                            col = T_KBHI + (kb - 4 * t - 4) * 8 + h
                        sp_["exp_op"] = P.add("act", lambda e, c=c, esl=esl, bk=bk, col=col: e.activation(
                            out=Et[c][esl], in_=ps[bk][:], func=AF.Exp,
                            bias=tabs[:, col:col + 1], scale=SCALE),
                            reads=["ps%d" % bk, "tabs"], writes=["E%d%d" % (c, esl)])

            def emit_pv(sp_):
                esl, hh, kb, ki, last = (sp_[k] for k in ("esl", "hh", "kb", "ki", "last"))
                for c in range(2):
                    for qb in range(4):
                        a = c * 4 + qb
                        bk = 4 + a // 3
                        o0 = (a % 3) * 130
                        P.add("pe", lambda e, c=c, qb=qb, bk=bk, o0=o0, esl=esl, kb=kb, hh=hh, ki=ki, a=a, last=last:
                              e.matmul(out=ps[bk][:, o0:o0 + 129], lhsT=Et[c][esl][:, qb * 128:(qb + 1) * 128],
                                       rhs=Vt[:, kb, hh, 0:129], start=(ki == 0 and a % 3 == 0), stop=last,
                                       skip_group_check=True),
                              reads=["E%d%d" % (c, esl), "V%d" % kb], writes=["ps%d" % bk])

            def emit_evac(sp_):
                h, t, cname, first_cat = (sp_[k] for k in ("h", "t", "cname", "first_cat"))
                ob = (sp_["hh"] * 4 + t) % 2
                Oacc = Oacc2[ob]
                for c in range(2):
                    for qb in range(4):
                        a = c * 4 + qb
                        bk = 4 + a // 3
                        o0 = (a % 3) * 130
                        if cname == "far":
                            fac = qf[:, (4 * t + qb) * 8 + h:(4 * t + qb) * 8 + h + 1]
                        elif cname == "lo":
                            fac = qf[:, 128 + qb * 8 + h:128 + qb * 8 + h + 1]
                        elif cname == "hi":
                            fac = qf[:, 160 + qb * 8 + h:160 + qb * 8 + h + 1]
                        else:
                            fac = None
                        if first_cat and fac is not None:
                            P.add("dve", lambda e, a=a, bk=bk, o0=o0, fac=fac: e.tensor_scalar(
                                out=Oacc[:, a, 0:129], in0=ps[bk][:, o0:o0 + 129], scalar1=fac, scalar2=None,
                                op0=ALU.mult),
                                reads=["ps%d" % bk, "qf"], writes=["Oacc%d_%d" % (ob, a)])
                        elif first_cat:
                            P.add("dve", lambda e, a=a, bk=bk, o0=o0: e.tensor_copy(
                                out=Oacc[:, a, 0:129], in_=ps[bk][:, o0:o0 + 129]),
                                reads=["ps%d" % bk], writes=["Oacc%d_%d" % (ob, a)])
                        elif fac is None:
                            P.add("dve", lambda e, a=a, bk=bk, o0=o0: e.tensor_tensor(
                                out=Oacc[:, a, 0:129], in0=ps[bk][:, o0:o0 + 129], in1=Oacc[:, a, 0:129],
                                op=ALU.add),
                                reads=["ps%d" % bk, "Oacc%d_%d" % (ob, a)], writes=["Oacc%d_%d" % (ob, a)])
                        else:
                            P.add("dve", lambda e, a=a, bk=bk, o0=o0, fac=fac: e.scalar_tensor_tensor(
                                out=Oacc[:, a, 0:129], in0=ps[bk][:, o0:o0 + 129], scalar=fac,
                                in1=Oacc[:, a, 0:129], op0=ALU.mult, op1=ALU.add),
                                reads=["ps%d" % bk, "Oacc%d_%d" % (ob, a), "qf"], writes=["Oacc%d_%d" % (ob, a)])

            def emit_finalize(sp_):
                h, t = sp_["h"], sp_["t"]
                ob = (sp_["hh"] * 4 + t) % 2
                Oacc = Oacc2[ob]
                allO = ["Oacc%d_%d" % (ob, a) for a in range(8)]
                Ov = Oacc.rearrange("p (c q) n -> p c q n", c=2)
                th = []
                th.append(lambda: P.add("dve", lambda e: e.reciprocal(out=rz.rearrange("p q c -> p c q"), in_=Ov[:, :, :, 128]),
                                        reads=allO, writes=["rz"]))
                th.append(lambda: P.add("dve", lambda e: e.tensor_scalar(out=rz1l, in0=rz[:, :, 1], scalar1=neglam[:, 0:1],
                                                                        scalar2=None, op0=ALU.mult),
                                        reads=["rz", "neglam"], writes=["rz1l"]))
                th.append(lambda: P.add("dve", lambda e: e.tensor_tensor(out=tA, in0=Ov[:, 0, :, 0:128],
                                                                        in1=rz[:, :, 0:1].to_broadcast([128, 4, 128]), op=ALU.mult),
                                        reads=allO + ["rz"], writes=["tA"]))
                th.append(lambda: P.add("dve", lambda e: e.tensor_tensor(out=tB, in0=Ov[:, 1, :, 0:128],
                                                                        in1=rz1l.unsqueeze(2).to_broadcast([128, 4, 128]), op=ALU.mult),
                                        reads=allO + ["rz1l"], writes=["tB"]))
                th.append(lambda: P.add("dve", lambda e: e.tensor_tensor(out=tA, in0=tA, in1=tB, op=ALU.add),
                                        reads=["tA", "tB"], writes=["tA"]))
                th.append(lambda: P.add("dve", lambda e: e.tensor_tensor(out=tB, in0=tA, in1=tA, op=ALU.mult),
                                        reads=["tA"], writes=["tB"]))
                th.append(lambda: P.add("dve", lambda e: e.tensor_reduce(out=ssum, in_=tB, axis=AX.X, op=ALU.add),
                                        reads=["tB"], writes=["ssum"]))
                th.append(lambda: P.add("act", lambda e: e.activation(out=rstd, in_=ssum, func=AF.Sqrt, bias=LN_EPS,
                                                                     scale=1.0 / 128.0),
                                        reads=["ssum"], writes=["rstd0"]))
                th.append(lambda: P.add("dve", lambda e: e.reciprocal(out=rstd, in_=rstd), reads=["rstd0"], writes=["rstd"]))
                th.append(lambda: P.add("dve", lambda e: e.tensor_tensor(out=tB, in0=tA,
                                                                        in1=rstd.unsqueeze(2).to_broadcast([128, 4, 128]), op=ALU.mult),
                                        reads=["tA", "rstd"], writes=["tB"]))
                th.append(lambda: P.add("dve", lambda e: e.tensor_tensor(out=ofb, in0=tB,
                                                                        in1=gainT[:].unsqueeze(1).to_broadcast([128, 4, 128]), op=ALU.mult),
                                        reads=["tB", "gainT"], writes=["ofb"]))
                pv = psTb[7][:, 0:512]

                def tr():
                    for qb in range(4):
                        P.add("pe", lambda e, qb=qb: e.transpose(
                            out=pv[:, qb * 128:(qb + 1) * 128], in_=ofb[:, qb, :], identity=ident[:]),
                            reads=["ofb", "ident"], writes=["ps7"])
                th.append(tr)
                th.append(lambda: P.add("dve", lambda e: e.tensor_copy(out=attnT[:, h, t * 512:(t + 1) * 512], in_=pv),
                                        reads=["ps7"], writes=["attnT%d_%d" % (h, t)]))
                return th

            pending = []
            ncast_g = cast_plan[g]
            cast_stride = max(1, len(steps) // max(1, ncast_g)) if ncast_g else 1
            emit_qk(steps[0])
            for i, sp_ in enumerate(steps):
                if i + 1 < len(steps):
                    emit_qk(steps[i + 1])
                emit_exp(sp_)
                emit_pv(sp_)
                if sp_["last"]:
                    emit_evac(sp_)
                if ncast_g and i % cast_stride == 0 and i // cast_stride < ncast_g:
                    emit_cast(sp_["exp_op"])
                for _ in range(3):
                    if pending:
                        pending.pop(0)()
                if sp_["final"]:
                    pending.extend(emit_finalize(sp_))
            while pending:
                pending.pop(0)()

        while cast_jobs:
            emit_cast(None)

        if debug == "attn":
            allA = ["attnT%d_%d" % (h, t) for h in range(8) for t in range(4)]
            P.add("sp", lambda e: e.dma_start(out=dbg, in_=attnT[:].rearrange("p a b -> p (a b)")),
                  reads=allA, writes=["dbgout"], dma="o_dbg")
            P.add("sp", lambda e: e.dma_start(out=out_v[:, 0, 0:128], in_=gainT[:]), reads=["gainT", "dbgout"],
                  writes=["outx"], dma="o_fin")
            P.add("sp", lambda e: e.nop(), reads=["outx", "dbgout"], writes=["fin"])
            P.emit()
            return nc

        P.barrier()
        cv = Carver()
        xs2 = cv.take([4, 2048], BF16)
        xT2 = cv.take([16, 512], BF16)
        r1o = cv.off
        x32 = cv.take([4, 2048], F32)
        r1e = cv.off
        cv.off = r1o
        sv = cv.take([4, 1024], F32)
        vn = cv.take([4, 1024], BF16)
        uT = cv.take([8, 512], BF16)
        assert cv.off <= r1e
        cv.off = r1e
        r2o = cv.off
        hT = cv.take([32, 512], BF16)
        r2e = cv.off
        cv.off = r2o
        mergedT = cv.take([16, 512], BF16)
        _save = cv.off
        cv.off = r2o
        sgG = cv.take([1024], F32)
        sgB = cv.take([1024], F32)
        BsT = cv.take([8, 128], F32)
        assert cv.off <= _save
        cv.off = _save
        sgoT = cv.take([8, 512], BF16)
        g0t = cv.take([512], F32)
        g1t = cv.take([512], F32)
        gtmp = cv.take([512], F32)
        assert cv.off <= r2e
        cv.off = r2e
        NWB = 3
        wb = [cv.take([8192], BF16) for _ in range(NWB)]
        lnT = cv.take([2048], F32)
        WsT = cv.take([8, 128], BF16)
        bgT = cv.take([32], F32)
        rtmp = [cv.take([512], F32) for _ in range(2)]
        stats = cv.take([4, 6], F32)
        mv = cv.take([2], F32)
        rs = cv.take([1], F32)

        R1A, R1B, R2A, R2B = "R1A", "R1B", "R2A", "R2B"
        SUB = ["a", "b", "c", "d"]

        P.add("sp", lambda e: e.dma_start(out=bgT, in_=b_gate), writes=["bgT"], dma="c_bgT")
        P.add("pool", lambda e: e.dma_start(out=WsT, in_=sg_w_s), writes=["WsT"], dma="c_WsT")

        st = {"wb": 0, "bank": 0, "rt": 0}

        def wslot():
            i = st["wb"] % NWB
            st["wb"] += 1
            return i

        def wkeys(i, n=4):
            return ["wb%d%s" % (i, SUB[k]) for k in range(n)]

        def bank6():
            b = st["bank"] % 6
            st["bank"] += 1
            return b

        def layer_norm_rows(buf, b, nfree, keyin, keyout, extra_reads):
            nch = nfree // 512
            for ch in range(nch):
                P.add("dve", lambda e, ch=ch: e.bn_stats(out=stats[:, ch, :], in_=buf[:, b, ch * 512:(ch + 1) * 512]),
                      reads=[keyin] + extra_reads, writes=["stats%d" % ch])
            P.add("dve", lambda e: e.bn_aggr(out=mv, in_=stats[:, 0:nch, :].rearrange("p a b -> p (a b)")),
                  reads=["stats%d" % ch for ch in range(nch)], writes=["mv"])
            P.add("act", lambda e: e.activation(out=rs, in_=mv[:, 1:2], func=AF.Sqrt, bias=LN_EPS, scale=1.0),
                  reads=["mv"], writes=["rs0"])
            P.add("dve", lambda e: e.reciprocal(out=rs, in_=rs), reads=["rs0"], writes=["rs"])
            P.add("dve", lambda e: e.tensor_scalar(out=buf[:, b, 0:nfree], in0=buf[:, b, 0:nfree], scalar1=mv[:, 0:1],
                                                   scalar2=rs[:, 0:1], op0=ALU.subtract, op1=ALU.mult),
                  reads=[keyin, "mv", "rs"] + extra_reads, writes=[keyout])

        for t in range(4):
            P.add("sp", lambda e: e.dma_start(out=sgG, in_=sg_ln_g[0, :].partition_broadcast(128)),
                  reads=[R2A], writes=["sgG", R2B], dma="c_sgG")
            P.add("sp", lambda e: e.dma_start(out=sgB, in_=sg_ln_b[0, :].partition_broadcast(128)),
                  reads=[R2A], writes=["sgB"], dma="c_sgB")
            P.add("sp", lambda e: e.dma_start(out=BsT.rearrange("p a b -> p (a b)"),
                                              in_=sg_b_s[0, :].partition_broadcast(128)),
                  reads=[R2A], writes=["BsT"], dma="c_BsT")
            P.add("sp", lambda e, t=t: e.dma_start(out=xT2.rearrange("p a b -> p (a b)"), in_=xTs[t]),
                  reads=["xTs%d" % t], writes=["p2xT%d" % c for c in range(16)], dma="p2xT")
            xTk = ["p2xT%d" % c for c in range(16)]
            firstA = True
            for jj in range(2):
                wi = wslot()
                P.add("pool", lambda e, jj=jj, wi=wi: e.dma_start(
                    out=wb[wi].rearrange("p (c n) -> p c n", c=16),
                    in_=w_in_v[:, :, U_OFF + jj * 512:U_OFF + (jj + 1) * 512]),
                    writes=wkeys(wi), dma="wb%d" % wi)
                wv = wb[wi].rearrange("p (c n) -> p c n", c=16)
                for j4 in range(4):
                    j = jj * 4 + j4
                    bk = bank6()
                    for c in range(16):
                        P.add("pe", lambda e, c=c, j4=j4, bk=bk, wv=wv: e.matmul(
                            out=ps[bk][:], lhsT=wv[:, c, j4 * 128:(j4 + 1) * 128], rhs=xT2[:, c, :],
                            start=(c == 0), stop=(c == 15)),
                            reads=["p2xT%d" % c] + wkeys(wi), writes=["ps%d" % bk])
                    P.add("act", lambda e, j=j, bk=bk: e.activation(out=uT[:, j, :], in_=ps[bk][:], func=AF.Gelu),
                          reads=["ps%d" % bk, R1A], writes=["uT%d" % j] + ([R1B] if firstA else []))
                    firstA = False
            for jj in range(2):
                wi = wslot()
                P.add("pool", lambda e, jj=jj, wi=wi: e.dma_start(
                    out=wb[wi].rearrange("p (c n) -> p c n", c=16),
                    in_=w_in_v[:, :, SGV_OFF + jj * 512:SGV_OFF + (jj + 1) * 512]),
                    writes=wkeys(wi), dma="wb%d" % wi)
                wv = wb[wi].rearrange("p (c n) -> p c n", c=16)
                for b in range(4):
                    bk = bank6()
                    for c in range(16):
                        P.add("pe", lambda e, c=c, b=b, bk=bk, wv=wv: e.matmul(
                            out=ps[bk][:], lhsT=xT2[:, c, b * 128:(b + 1) * 128], rhs=wv[:, c, :],
                            start=(c == 0), stop=(c == 15)),
                            reads=["p2xT%d" % c] + wkeys(wi), writes=["ps%d" % bk])
                    P.add("act", lambda e, b=b, jj=jj, bk=bk: e.activation(
                        out=sv[:, b, jj * 512:(jj + 1) * 512], in_=ps[bk][:], func=AF.Gelu),
                        reads=["ps%d" % bk, R1A], writes=["sv%d_%d" % (b, jj)])
            for b in range(4):
                layer_norm_rows(sv, b, 1024, "sv%d_0" % b, "svn%d" % b, [R1A, "sv%d_1" % b])
                P.add("dve", lambda e, b=b: e.tensor_tensor(out=sv[:, b, :], in0=sv[:, b, :], in1=sgG, op=ALU.mult),
                      reads=["svn%d" % b, "sgG", R1A], writes=["svg%d" % b])
                P.add("dve", lambda e, b=b: e.tensor_tensor(out=vn[:, b, :], in0=sv[:, b, :], in1=sgB, op=ALU.add),
                      reads=["svg%d" % b, "sgB", R1A], writes=["vn%d" % b])
            firstA2 = True
            for b in range(4):
                for gh in range(2):
                    bk = bank6()
                    for gi in range(4):
                        g_ = gh * 4 + gi
                        P.add("pe", lambda e, b=b, gi=gi, g_=g_, bk=bk: e.matmul(
                            out=ps[bk][:, gi * 128:(gi + 1) * 128], lhsT=vn[:, b, g_ * 128:(g_ + 1) * 128],
                            rhs=WsT[:, g_, :], start=True, stop=True),
                            reads=["vn%d" % b, "WsT", R1A], writes=["ps%d" % bk])
                    P.add("dve", lambda e, gh=gh, bk=bk: e.tensor_tensor(
                        out=gtmp.rearrange("p (a b) -> p a b", a=4), in0=ps[bk][:].rearrange("p (a b) -> p a b", a=4),
                        in1=BsT[:, gh * 4:gh * 4 + 4, :], op=ALU.add),
                        reads=["ps%d" % bk, "BsT", R2A], writes=["gtmp"] + ([R2B] if firstA2 else []))
                    firstA2 = False
                    P.add("dve", lambda e, gh=gh, b=b: e.tensor_tensor(
                        out=sgoT[:, gh * 4:gh * 4 + 4, b * 128:(b + 1) * 128],
                        in0=gtmp.rearrange("p (a b) -> p a b", a=4),
                        in1=uT[:, gh * 4:gh * 4 + 4, b * 128:(b + 1) * 128], op=ALU.mult),
                        reads=["gtmp", R1A, R2A] + ["uT%d" % (gh * 4 + k) for k in range(4)],
                        writes=["sgoT%d_%d" % (gh, b)])
            sgok = ["sgoT%d_%d" % (gh, b) for gh in range(2) for b in range(4)]
            for j in range(16):
                wi = wslot()
                w = wb[wi]
                wg0 = w[:, 0:2048].rearrange("p (c n) -> p c n", c=16)
                wg1 = w[:, 2048:4096].rearrange("p (c n) -> p c n", c=16)
                wa = w[:, 4096:5120].rearrange("p (c n) -> p c n", c=8)
                wsg = w[:, 5120:6144].rearrange("p (c n) -> p c n", c=8)
                wk = wkeys(wi)
                P.add("pool", lambda e, j=j, wg0=wg0: e.dma_start(
                    out=wg0, in_=w_in_v[:, :, GATE_OFF + j * 128:GATE_OFF + (j + 1) * 128]),
                    writes=[wk[0]], dma="wb%da" % wi)
                P.add("pool", lambda e, j=j, wg1=wg1: e.dma_start(
                    out=wg1, in_=w_in_v[:, :, GATE_OFF + D + j * 128:GATE_OFF + D + (j + 1) * 128]),
                    writes=[wk[1]], dma="wb%db" % wi)
                P.add("pool", lambda e, j=j, wa=wa: e.dma_start(out=wa, in_=w_ba_v[:, :, j * 128:(j + 1) * 128]),
                      writes=[wk[2]], dma="wb%dc" % wi)
                P.add("pool", lambda e, j=j, wsg=wsg: e.dma_start(out=wsg, in_=w_bs_v[:, :, j * 128:(j + 1) * 128]),
                      writes=[wk[3]], dma="wb%dd" % wi)
                bks = [bank6() for _ in range(4)]
                for gi, (wgx, gt) in enumerate(((wg0, g0t), (wg1, g1t))):
                    bk = bks[gi]
                    for c in range(16):
                        P.add("pe", lambda e, c=c, bk=bk, wgx=wgx: e.matmul(
                            out=ps[bk][:], lhsT=wgx[:, c, :], rhs=xT2[:, c, :], start=(c == 0), stop=(c == 15)),
                            reads=["p2xT%d" % c, wk[gi]], writes=["ps%d" % bk])
                    P.add("act", lambda e, bk=bk, gt=gt, gi=gi, j=j: e.activation(
                        out=gt, in_=ps[bk][:], func=AF.Sigmoid, bias=bgT[:, gi * 16 + j:gi * 16 + j + 1], scale=1.0),
                        reads=["ps%d" % bk, "bgT", R2A], writes=["g%dt" % gi])
                for c in range(8):
                    P.add("pe", lambda e, c=c, bk=bks[2], wa=wa, t=t: e.matmul(
                        out=ps[bk][:], lhsT=wa[:, c, :], rhs=attnT[:, c, t * 512:(t + 1) * 512],
                        start=(c == 0), stop=(c == 7)),
                        reads=[wk[2]], writes=["ps%d" % bks[2]])
                for c in range(8):
                    P.add("pe", lambda e, c=c, bk=bks[3], wsg=wsg: e.matmul(
                        out=ps[bk][:], lhsT=wsg[:, c, :], rhs=sgoT[:, c, :], start=(c == 0), stop=(c == 7)),
                        reads=[wk[3], R2A] + sgok, writes=["ps%d" % bks[3]])
                P.add("dve", lambda e, bk=bks[2]: e.tensor_tensor(out=g0t, in0=g0t, in1=ps[bk][:], op=ALU.mult),
                      reads=["g0t", "ps%d" % bks[2], R2A], writes=["g0m"])
                P.add("dve", lambda e, bk=bks[3]: e.tensor_tensor(out=g1t, in0=g1t, in1=ps[bk][:], op=ALU.mult),
                      reads=["g1t", "ps%d" % bks[3], R2A], writes=["g1m"])
                P.add("dve", lambda e, j=j: e.tensor_tensor(out=mergedT[:, j, :], in0=g0t, in1=g1t, op=ALU.add),
                      reads=["g0m", "g1m", R2A], writes=["mg%d" % j, "g0t", "g1t", "sgG", "sgB", "BsT"])
            mgk = ["mg%d" % j for j in range(16)]
            P.add("sp", lambda e, t=t: e.dma_start(out=x32, in_=x_v[:, 4 * t:4 * t + 4, :]),
                  reads=[R1B], writes=["r%d" % b for b in range(4)] + [R1A], dma="x32")
            for ct in range(4):
                wi = wslot()
                P.add("pool", lambda e, ct=ct, wi=wi: e.dma_start(
                    out=wb[wi].rearrange("p (c n) -> p c n", c=16), in_=w_o_v[:, :, ct * 512:(ct + 1) * 512]),
                    writes=wkeys(wi), dma="wb%d" % wi)
                wv = wb[wi].rearrange("p (c n) -> p c n", c=16)
                for b in range(4):
                    bk = bank6()
                    for c in range(16):
                        P.add("pe", lambda e, c=c, b=b, bk=bk, wv=wv: e.matmul(
                            out=ps[bk][:], lhsT=mergedT[:, c, b * 128:(b + 1) * 128], rhs=wv[:, c, :],
                            start=(c == 0), stop=(c == 15)),
                            reads=["mg%d" % c, R2A] + wkeys(wi), writes=["ps%d" % bk])
                    P.add("dve", lambda e, b=b, ct=ct, bk=bk: e.scalar_tensor_tensor(
                        out=x32[:, b, ct * 512:(ct + 1) * 512], in0=x32[:, b, ct * 512:(ct + 1) * 512], scalar=ALPHA,
                        in1=ps[bk][:], op0=ALU.mult, op1=ALU.add),
                        reads=["ps%d" % bk, "r%d" % b, R1B], writes=["r%d" % b])
            for (gsrc, bsrc, stage) in ((ln1_g, ln1_b, 1),):
                P.add("sp", lambda e, gsrc=gsrc: e.dma_start(out=lnT, in_=gsrc[0, :].partition_broadcast(128)),
                      writes=["lnT"], dma="lnT")
                for b in range(4):
                    layer_norm_rows(x32, b, 2048, "r%d" % b, "rn%d" % b, [R1B])
                    P.add("dve", lambda e, b=b: e.tensor_tensor(out=x32[:, b, :], in0=x32[:, b, :], in1=lnT, op=ALU.mult),
                          reads=["rn%d" % b, "lnT", R1B], writes=["rg%d" % b])
                P.add("sp", lambda e, bsrc=bsrc: e.dma_start(out=lnT, in_=bsrc[0, :].partition_broadcast(128)),
                      writes=["lnT"], dma="lnT")
                for b in range(4):
                    P.add("dve", lambda e, b=b: e.tensor_tensor(out=x32[:, b, :], in0=x32[:, b, :], in1=lnT, op=ALU.add),
                          reads=["rg%d" % b, "lnT", R1B], writes=["x1_%d" % b])
                    P.add("act", lambda e, b=b: e.activation(out=xs2[:, b, :], in_=x32[:, b, :], func=AF.Copy),
                          reads=["x1_%d" % b, R1B], writes=["p2xs"])
            transpose_tile(xs2, "p2xs", xT2, "p2xT%d")
            for fh in range(2):
                firstB2 = True
                for f4 in range(8):
                    wi = wslot()
                    c0 = (fh * 32 + f4 * 4) * 128
                    kup = fh * 8 + f4
                    P.add("sp", lambda e, kup=kup, wi=wi: e.dma_start(out=wb[wi], in_=scr_up[kup]),
                          reads=["scr_up%d" % kup], writes=wkeys(wi), dma="wbs%d" % wi)
                    wv = wb[wi].rearrange("p (c n) -> p c n", c=16)
                    for fi in range(4):
                        bk = bank6()
                        for c in range(16):
                            P.add("pe", lambda e, c=c, fi=fi, bk=bk, wv=wv: e.matmul(
                                out=ps[bk][:], lhsT=wv[:, c, fi * 128:(fi + 1) * 128], rhs=xT2[:, c, :],
                                start=(c == 0), stop=(c == 15)),
                                reads=["p2xT%d" % c] + wkeys(wi), writes=["ps%d" % bk])
                        ri = st["rt"] % 2
                        st["rt"] += 1
                        P.add("act", lambda e, bk=bk, ri=ri: e.activation(out=rtmp[ri], in_=ps[bk][:], func=AF.Relu),
                              reads=["ps%d" % bk], writes=["rtmp%d" % ri])
                        kk = f4 * 4 + fi
                        P.add("dve", lambda e, ri=ri, kk=kk: e.tensor_tensor(out=hT[:, kk, :], in0=rtmp[ri], in1=rtmp[ri],
                                                                          op=ALU.mult),
                              reads=["rtmp%d" % ri, R2B], writes=["hT%d" % kk] + ([R2A] if firstB2 else []))
                        firstB2 = False
                for ct in range(4):
                    bset = [0, 1, 2, 3] if ct % 2 == 0 else [4, 5, 6, 7]
                    for fp in range(2):
                        wi = wslot()
                        r0 = fh * 32 + fp * 16
                        kdn = (fh * 2 + fp) * 4 + ct
                        P.add("sp", lambda e, kdn=kdn, wi=wi: e.dma_start(out=wb[wi], in_=scr_dn[kdn]),
                              reads=["scr_dn%d" % kdn], writes=wkeys(wi), dma="wbs%d" % wi)
                        wv = wb[wi].rearrange("p (c n) -> p c n", c=16)
                        for b in range(4):
                            bk = bset[b]
                            for k in range(16):
                                P.add("pe", lambda e, k=k, b=b, bk=bk, wv=wv, fp=fp: e.matmul(
                                    out=ps[bk][:], lhsT=hT[:, fp * 16 + k, b * 128:(b + 1) * 128], rhs=wv[:, k, :],
                                    start=(fp == 0 and k == 0), stop=(fp == 1 and k == 15)),
                                    reads=["hT%d" % (fp * 16 + k), R2B] + wkeys(wi), writes=["ps%d" % bk])
                    for b in range(4):
                        bk = bset[b]
                        if fh == 0:
                            P.add("dve", lambda e, b=b, ct=ct, bk=bk: e.scalar_tensor_tensor(
                                out=x32[:, b, ct * 512:(ct + 1) * 512], in0=x32[:, b, ct * 512:(ct + 1) * 512],
                                scalar=ALPHA, in1=ps[bk][:], op0=ALU.mult, op1=ALU.add),
                                reads=["ps%d" % bk, "x1_%d" % b, R1B], writes=["x1_%d" % b])
                        else:
                            P.add("dve", lambda e, b=b, ct=ct, bk=bk: e.tensor_tensor(
                                out=x32[:, b, ct * 512:(ct + 1) * 512], in0=x32[:, b, ct * 512:(ct + 1) * 512],
                                in1=ps[bk][:], op=ALU.add),
                                reads=["ps%d" % bk, "x1_%d" % b, R1B], writes=["x1_%d" % b])
            P.add("sp", lambda e: e.dma_start(out=lnT, in_=ln2_g[0, :].partition_broadcast(128)),
                  writes=["lnT"], dma="lnT")
            for b in range(4):
                layer_norm_rows(x32, b, 2048, "x1_%d" % b, "y2n%d" % b, [R1B])
                P.add("dve", lambda e, b=b: e.tensor_tensor(out=x32[:, b, :], in0=x32[:, b, :], in1=lnT, op=ALU.mult),
                      reads=["y2n%d" % b, "lnT", R1B], writes=["y2g%d" % b])
            P.add("sp", lambda e: e.dma_start(out=lnT, in_=ln2_b[0, :].partition_broadcast(128)),
                  writes=["lnT"], dma="lnT")
            for b in range(4):
                P.add("dve", lambda e, b=b: e.tensor_tensor(out=x32[:, b, :], in0=x32[:, b, :], in1=lnT, op=ALU.add),
                      reads=["y2g%d" % b, "lnT", R1B], writes=["r%d" % b])
                P.add("sp", lambda e, b=b, t=t: e.dma_start(out=out_v[:, 4 * t + b, :], in_=x32[:, b, :]),
                      reads=["r%d" % b, R1B], writes=["out%d" % (4 * t + b)], dma="o_out")
        P.add("sp", lambda e: e.nop(), reads=["out%d" % i for i in range(16)], writes=["fin"])

        P.emit()
    return nc


def make_tabs(half):
    p = np.arange(128, dtype=np.float64)[:, None]
    tabs = np.zeros((128, T_END), dtype=np.float64)
    m = np.array(SLOPES, dtype=np.float64)[None, :]
    for j in range(16):
        if half == 0:
            dk = (2048 + 128 * j + p) - 2047
        else:
            dk = 2048 - (128 * (15 - j) + p)
        tabs[:, T_KBFAR + j * 8:T_KBFAR + j * 8 + 8] = -m * dk
        if half == 0:
            dq = 2047 - (128 * j + p)
        else:
            dq = (128 * j + p)
        tabs[:, T_QFAR + j * 8:T_QFAR + j * 8 + 8] = -m * dq
    for Dd in range(1, 13):
        tabs[:, T_KBLO + (Dd - 1) * 8:T_KBLO + Dd * 8] = -m * (128 * Dd - p)
    for Dd in range(12):
        tabs[:, T_KBHI + Dd * 8:T_KBHI + Dd * 8 + 8] = -m * (128 * Dd + p + 1)
    for qb in range(4):
        tabs[:, T_QLO + qb * 8:T_QLO + qb * 8 + 8] = -m * (128 * qb + p)
        tabs[:, T_QHI + qb * 8:T_QHI + qb * 8 + 8] = -m * (511 - 128 * qb - p)
    xx = np.arange(896, dtype=np.float64)[None, :]
    tabs[:, T_TW:T_END] = np.abs(xx - p - 384)
    return tabs.astype(np.float32)


def make_in_maps(inputs):
    f = lambda a: np.ascontiguousarray(np.asarray(a, dtype=np.float32))
    x = f(inputs["x"])
    common = {
        "w_in": f(inputs["w_in"])[0],
        "lam4": np.stack([f(inputs["lambda_q1"])[0], f(inputs["lambda_k1"])[0],
                          f(inputs["lambda_q2"])[0], f(inputs["lambda_k2"])[0]]),
        "subln_g": f(inputs["attn_subln_g"]).reshape(1, 128),
        "sg_ln_g": f(inputs["sg_ln_g"]).reshape(1, 1024),
        "sg_ln_b": f(inputs["sg_ln_b"]).reshape(1, 1024),
        "sg_w_s": np.ascontiguousarray(f(inputs["sg_w_s"])[0].transpose(2, 0, 1)),
        "sg_b_s": f(inputs["sg_b_s"]).reshape(1, 1024),
        "b_gate": np.ascontiguousarray(f(inputs["b_gate"]).reshape(32, 128).T),
        "w_ba": f(inputs["w_branch_attn"])[0],
        "w_bs": f(inputs["w_branch_sg"])[0],
        "w_o": f(inputs["w_o"])[0],
        "ln1_g": f(inputs["ln1_g"]).reshape(1, D),
        "ln1_b": f(inputs["ln1_b"]).reshape(1, D),
        "w_up": f(inputs["w_up"])[0],
        "w_down": f(inputs["w_down"])[0],
        "ln2_g": f(inputs["ln2_g"]).reshape(1, D),
        "ln2_b": f(inputs["ln2_b"]).reshape(1, D),
        "ident": np.eye(128, dtype=np.float32),
    }
    maps = []
    for c in range(8):
        b, half = c // 2, c % 2
        xo = x[b, half * TOWN:(half + 1) * TOWN]
        xt = x[b, (1 - half) * TOWN:(2 - half) * TOWN]
        mp = dict(common)
        if half == 1:
            xt = xt.reshape(16, 128, D)[::-1].reshape(TOWN, D)
        mp["x_perm"] = np.ascontiguousarray(np.concatenate([xo, xt], axis=0))
        mp["tabs"] = make_tabs(half)
        maps.append(mp)
    return maps


def kernel(**inputs):
    nc = build_nc()
    maps = make_in_maps(inputs)
    res = run_bass_kernel_spmd(nc, maps, core_ids=list(range(8)))
    outp = np.zeros((4, SEQ, D), dtype=np.float32)
    for c in range(8):
        b, half = c // 2, c % 2
        outp[b, half * TOWN:(half + 1) * TOWN] = res.results[c]["out"]
    return outp
```

```python
import math
from contextlib import ExitStack

import numpy as np
import concourse.bass as bass
import concourse.mybir as mybir
from concourse.bass_utils import run_bass_kernel_spmd

F32 = mybir.dt.float32
BF16 = mybir.dt.bfloat16
AF = mybir.ActivationFunctionType
ALU = mybir.AluOpType
AX = mybir.AxisListType

D = 2048
SEQ = 4096
TOWN = 2048
NH = 8
Q_OFF, K_OFF, V_OFF, U_OFF, SGV_OFF, GATE_OFF = 0, 1024, 2048, 3072, 4096, 5120
IN_W = 9216
DFF = 8192
ALPHA = 2.0 ** 0.25
LN_EPS = 1e-5
LAMBDA_INIT = 0.8 - 0.6 * math.exp(0.0)
SLOPES = [2.0 ** (-(h + 1)) for h in range(NH)]
SCALE = 0.125

T_KBFAR, T_KBLO, T_KBHI, T_QFAR, T_QLO, T_QHI, T_TW, T_END = 0, 128, 224, 320, 448, 480, 512, 1408


class Op:
    __slots__ = ("eng", "fn", "signal", "sigval", "deps", "sem", "is_dma", "idx")


class Prog:
    ENGS = ("pe", "act", "dve", "pool", "sp")

    def __init__(self, nc, es):
        self.nc = nc
        self.es = es
        self.ops = {e: [] for e in self.ENGS}
        self.last_w = {}
        self.readers = {}
        self.esem = {e: es.enter_context(nc.semaphore("sem_" + e)) for e in self.ENGS}
        self.dma_sems = {}
        self.dma_cnt = {}

    def dsem(self, name):
        if name not in self.dma_sems:
            self.dma_sems[name] = self.es.enter_context(self.nc.semaphore("dq_" + name))
            self.dma_cnt[name] = 0
        return name

    def add(self, eng, fn, reads=(), writes=(), dma=None, after=()):
        op = Op()
        op.eng, op.fn, op.signal, op.sigval, op.deps = eng, fn, False, 0, {}
        op.is_dma = dma is not None
        op.sem = None
        if op.is_dma:
            self.dsem(dma)
            op.sem = dma
            self.dma_cnt[dma] += 16
            op.sigval = self.dma_cnt[dma]
            op.signal = True

        def need(prev, raw):
            if prev is None or prev is op:
                return
            if not (prev.is_dma or op.is_dma):
                if prev.eng == eng:
                    if eng == "pe":
                        return
            prev.signal = True
            op.deps[id(prev)] = prev

        for a_ in after:
            need(a_, True)
        for r in reads:
            need(self.last_w.get(r), True)
        for w in writes:
            need(self.last_w.get(w), False)
            for rd in self.readers.get(w, {}).values():
                need(rd, False)
        skey = ("d", op.sem) if op.is_dma else ("e", eng)
        for r in reads:
            self.readers.setdefault(r, {})[skey] = op
        for w in writes:
            self.last_w[w] = op
            self.readers[w] = {}
        self.ops[eng].append(op)
        return op

    def barrier(self):
        lasts = []
        for e in self.ENGS:
            for op in reversed(self.ops[e]):
                if not op.is_dma:
                    lasts.append(op)
                    break
        seen = {}
        for e in self.ENGS:
            for op in self.ops[e]:
                if op.is_dma:
                    seen[op.sem] = op
        lasts += list(seen.values())
        for e in self.ENGS:
            op = Op()
            op.eng, op.fn, op.signal, op.sigval, op.deps = e, (lambda h: h.nop()), False, 0, {}
            op.is_dma, op.sem = False, None
            for l in lasts:
                if (not l.is_dma) and l.eng == e:
                    continue
                l.signal = True
                op.deps[id(l)] = l
            self.ops[e].append(op)

    def emit(self):
        nc = self.nc
        for e in self.ENGS:
            cnt = 0
            for op in self.ops[e]:
                if op.is_dma:
                    continue
                if op.signal:
                    cnt += 1
                    op.sigval = cnt

        def semof(op):
            if op.is_dma:
                return ("d", op.sem), self.dma_sems[op.sem]
            return ("e", op.eng), self.esem[op.eng]

        def run(e, handle):
            waited = {}
            for op in self.ops[e]:
                need = {}
                for d in op.deps.values():
                    k, s = semof(d)
                    if need.get(k, (None, 0))[1] < d.sigval:
                        need[k] = (s, d.sigval)
                for k, (s, v) in need.items():
                    if waited.get(k, 0) < v:
                        handle.wait_ge(s, v)
                        waited[k] = v
                ins = op.fn(handle)
                if op.signal:
                    if op.is_dma:
                        ins.then_inc(self.dma_sems[op.sem], 16)
                    else:
                        ins.then_inc(self.esem[e], 1)

        with nc.Block() as block:
            @block.tensor
            def _(h):
                run("pe", h)

            @block.scalar
            def _(h):
                run("act", h)

            @block.vector
            def _(h):
                run("dve", h)

            @block.gpsimd
            def _(h):
                run("pool", h)

            @block.sync
            def _(h):
                run("sp", h)


def build_nc(debug=None):
    nc = bass.Bass("TRN2", target_bir_lowering=False)
    dt_in = lambda name, shape: nc.dram_tensor(name, shape, F32, kind="ExternalInput").ap()
    x_perm = dt_in("x_perm", [SEQ, D])
    w_in = dt_in("w_in", [D, IN_W])
    lam4 = dt_in("lam4", [4, 64])
    subln_g = dt_in("subln_g", [1, 128])
    sg_ln_g = dt_in("sg_ln_g", [1, 1024])
    sg_ln_b = dt_in("sg_ln_b", [1, 1024])
    sg_w_s = dt_in("sg_w_s", [128, 8, 128])
    sg_b_s = dt_in("sg_b_s", [1, 1024])
    b_gate = dt_in("b_gate", [128, 32])
    w_ba = dt_in("w_ba", [1024, D])
    w_bs = dt_in("w_bs", [1024, D])
    w_o = dt_in("w_o", [D, D])
    ln1_g = dt_in("ln1_g", [1, D])
    ln1_b = dt_in("ln1_b", [1, D])
    w_up = dt_in("w_up", [D, DFF])
    w_down = dt_in("w_down", [DFF, D])
    ln2_g = dt_in("ln2_g", [1, D])
    ln2_b = dt_in("ln2_b", [1, D])
    tabs_d = dt_in("tabs", [128, T_END])
    ident_d = dt_in("ident", [128, 128])
    out = nc.dram_tensor("out", [TOWN, D], F32, kind="ExternalOutput").ap()
    dbg = None
    if debug is not None:
        dbg = nc.dram_tensor("dbg", [128, NH * TOWN], BF16, kind="ExternalOutput").ap()

    xTs = nc.dram_tensor("xTs", [8, 128, 8192], BF16).ap()
    scr_up = nc.dram_tensor("scr_up", [16, 128, 8192], BF16).ap()
    scr_dn = nc.dram_tensor("scr_dn", [16, 128, 8192], BF16).ap()
    x_v = x_perm.rearrange("(b p) d -> p b d", p=128)
    w_in_v = w_in.rearrange("(c p) n -> p c n", p=128)
    w_ba_v = w_ba.rearrange("(c p) n -> p c n", p=128)
    w_bs_v = w_bs.rearrange("(c p) n -> p c n", p=128)
    w_o_v = w_o.rearrange("(c p) n -> p c n", p=128)
    w_up_v = w_up.rearrange("(c p) n -> p c n", p=128)
    w_dn_v = w_down.rearrange("(c p) n -> p c n", p=128)
    out_v = out.rearrange("(b p) d -> p b d", p=128)

    with ExitStack() as es:
        P = Prog(nc, es)

        def sb(name, shape, dt):
            return es.enter_context(nc.sbuf_tensor("sb_" + name, shape, dt))

        ps = [es.enter_context(nc.psum_tensor("ps%d" % i, [128, 512], F32)) for i in range(8)]
        psTb = {6: ps[6][:].bitcast(BF16), 7: ps[7][:].bitcast(BF16)}

        ident = sb("ident", [128, 128], BF16)
        tabs = sb("tabs", [128, T_END], F32)
        qf = sb("qf", [128, 192], F32)
        lamt = sb("lamt", [128, 4, 64], F32)
        lprod = sb("lprod", [128, 2, 64], F32)
        lsum = sb("lsum", [128, 2], F32)
        lexp = sb("lexp", [128, 2], F32)
        neglam = sb("neglam", [128, 1], F32)
        gainT = sb("gainT", [128, 128], F32)
        attnT = sb("attnT", [128, NH, TOWN], BF16)
        ARENA = 80 * 1024
        arena = sb("arena", [128, ARENA], BF16)

        class Carver:
            def __init__(self):
                self.off = 0

            def take(self, shape, dt):
                n = int(np.prod(shape))
                units = n * (2 if dt == F32 else 1)
                units = (units + 15) // 16 * 16
                a = arena[:, self.off:self.off + units]
                self.off += units
                assert self.off <= ARENA, self.off
                if dt == F32:
                    a = a.bitcast(F32)
                a = a[:, 0:n]
                if len(shape) == 2:
                    return a.rearrange("p (a b) -> p a b", a=shape[0])
                if len(shape) == 3:
                    return a.rearrange("p (a b c) -> p a b c", a=shape[0], b=shape[1])
                return a

        P.add("pool", lambda e: e.dma_start(out=ident[:], in_=ident_d), writes=["ident"], dma="c_ident")
        P.add("sp", lambda e: e.dma_start(out=tabs[:], in_=tabs_d), writes=["tabs"], dma="c_tabs")
        P.add("sp", lambda e: e.dma_start(out=lamt[:].rearrange("p a b -> p (a b)"),
                                          in_=lam4.rearrange("a b -> (a b)").partition_broadcast(128)),
              writes=["lamt"], dma="c_lam")
        P.add("sp", lambda e: e.dma_start(out=gainT[:], in_=subln_g[0, :].partition_broadcast(128)),
              writes=["gainT"], dma="c_gain")
        P.add("act", lambda e: e.activation(out=qf[:], in_=tabs[:, T_QFAR:T_TW], func=AF.Exp),
              reads=["tabs"], writes=["qf"])
        P.add("dve", lambda e: e.tensor_tensor(out=lprod[:], in0=lamt[:, 0:4:2, :], in1=lamt[:, 1:4:2, :], op=ALU.mult),
              reads=["lamt"], writes=["lprod"])
        P.add("dve", lambda e: e.tensor_reduce(out=lsum[:], in_=lprod[:], axis=AX.X, op=ALU.add),
              reads=["lprod"], writes=["lsum"])
        P.add("act", lambda e: e.activation(out=lexp[:], in_=lsum[:], func=AF.Exp), reads=["lsum"], writes=["lexp"])
        P.add("dve", lambda e: e.tensor_tensor(out=neglam[:], in0=lexp[:, 1:2], in1=lexp[:, 0:1], op=ALU.subtract),
              reads=["lexp"], writes=["neglam0"])
        P.add("dve", lambda e: e.tensor_scalar(out=neglam[:], in0=neglam[:], scalar1=-LAMBDA_INIT, scalar2=None, op0=ALU.add),
              reads=["neglam0"], writes=["neglam"])
        P.add("dve", lambda e: e.tensor_scalar(out=gainT[:], in0=gainT[:], scalar1=1.0 - LAMBDA_INIT, scalar2=None, op0=ALU.mult),
              reads=["gainT"], writes=["gainT"])

        def early(reads, src_bf16=None):
            if src_bf16 is not None:
                P.add("sp", lambda e: e.dma_start(out=dbg[:, 0:src_bf16.shape[1]], in_=src_bf16), reads=reads,
                      writes=["dbgout"], dma="o_dbg")
            P.add("sp", lambda e: e.dma_start(out=out_v[:, 0, 0:128], in_=gainT[:]), reads=["gainT", "neglam", "qf"],
                  writes=["outx"], dma="o_fin")
            P.add("sp", lambda e: e.nop(), reads=["outx", "dbgout"], writes=["fin"])
            P.emit()
            return nc

        if debug == "setup":
            return early([])

        cv = Carver()
        Wg = [cv.take([16, 768], BF16) for _ in range(1)]
        xs = [cv.take([4, 2048], BF16) for _ in range(2)]
        xTb = [cv.take([16, 512], BF16) for _ in range(2)]
        QT = cv.take([2, TOWN], BF16)
        KT = cv.take([2, SEQ], BF16)
        Vt = cv.take([32, 2, 130], BF16)
        Et = [[cv.take([512], BF16) for _ in range(2)] for _ in range(2)]
        Oacc2 = [cv.take([8, 130], F32) for _ in range(2)]
        sdiag = [cv.take([512], F32) for _ in range(2)]
        rz = cv.take([4, 2], F32)
        rz1l = cv.take([4], F32)
        tA = cv.take([4, 128], F32)
        tB = cv.take([4, 128], F32)
        ssum = cv.take([4], F32)
        rstd = cv.take([4], F32)
        ofb = cv.take([4, 128], BF16)

        P.add("dve", lambda e: e.memset(Vt[:, :, :, 128:130], 1.0), writes=["Vones"])

        rot = {"ps": 0, "psT": 0, "xs": 0, "E": 0}

        def load_x_tile(ti):
            slot = rot["xs"] % 2
            rot["xs"] += 1
            P.add("pool", lambda e: e.dma_start(out=xs[slot], in_=x_v[:, 4 * ti:4 * ti + 4, :]),
                  writes=["xs%d" % slot], dma="xs%d" % slot)
            return slot

        def transpose_tile(src, src_key, dst, dst_key_fmt, evac="dve"):
            for c in range(16):
                sl = 6 + rot["psT"] % 2
                rot["psT"] += 1
                pv = psTb[sl][:, 0:512]
                for b in range(4):
                    P.add("pe", lambda e, b=b, c=c, pv=pv: e.transpose(
                        out=pv[:, b * 128:(b + 1) * 128], in_=src[:, b, c * 128:(c + 1) * 128], identity=ident[:]),
                        reads=[src_key, "ident"], writes=["ps%d" % sl])
                if evac == "dve":
                    P.add("dve", lambda e, c=c, pv=pv: e.tensor_copy(out=dst[:, c, :], in_=pv),
                          reads=["ps%d" % sl], writes=[dst_key_fmt % c])
                else:
                    P.add("act", lambda e, c=c, pv=pv: e.activation(out=dst[:, c, :], in_=pv, func=AF.Copy),
                          reads=["ps%d" % sl], writes=[dst_key_fmt % c])

        def nbank():
            b = rot["ps"] % 4
            rot["ps"] += 1
            return b

        cast_jobs = []
        for k in range(16):
            cast_jobs.append(("scr_up%d" % k, scr_up[k].rearrange("p (c n) -> p c n", c=16),
                              w_up_v[:, :, k * 512:(k + 1) * 512]))
        for kd in range(16):
            fhp, ct_ = kd // 4, kd % 4
            r0_ = (fhp // 2) * 32 + (fhp % 2) * 16
            cast_jobs.append(("scr_dn%d" % kd, scr_dn[kd].rearrange("p (c n) -> p c n", c=16),
                              w_dn_v[:, r0_:r0_ + 16, ct_ * 512:(ct_ + 1) * 512]))
        cast_plan = {0: 0, 1: 10, 2: 11, 3: 11}

        def emit_cast(pace_op):
            if not cast_jobs:
                return
            key, dst, src = cast_jobs.pop(0)
            P.add("pool", lambda e, dst=dst, src=src: e.dma_start(out=dst, in_=src),
                  writes=[key], dma="cast", after=([pace_op] if pace_op is not None else []))

        for g in range(4):
            wsl = 0
            wt = Wg[wsl]
            for i, off in enumerate((Q_OFF, K_OFF, V_OFF)):
                P.add("pool", lambda e, i=i, off=off, wt=wt, g=g: e.dma_start(
                    out=wt[:, :, i * 256:(i + 1) * 256], in_=w_in_v[:, :, off + 256 * g: off + 256 * g + 256]),
                    writes=["Wg%d_%d" % (wsl, i)], dma="Wg%d_%d" % (wsl, i))
            xslots = {}
            if g == 0:
                xslots[0] = load_x_tile(0)
            for ti in range(8):
                xsl = (g * 8 + ti) % 2
                xT = xTb[xsl]
                xkf = "xT%d_" % xsl + "%d"
                if g == 0:
                    if ti + 1 < 8:
                        xslots[ti + 1] = load_x_tile(ti + 1)
                    sl = xslots[ti]
                    transpose_tile(xs[sl], "xs%d" % sl, xT, xkf)
                    P.add("sp", lambda e, ti=ti, xT=xT: e.dma_start(out=xTs[ti], in_=xT.rearrange("p a b -> p (a b)")),
                          reads=[xkf % c for c in range(16)], writes=["xTs%d" % ti], dma="xTs_w%d" % xsl)
                else:
                    P.add("sp", lambda e, ti=ti, xT=xT: e.dma_start(out=xT.rearrange("p a b -> p (a b)"), in_=xTs[ti]),
                          reads=["xTs%d" % ti], writes=[xkf % c for c in range(16)], dma="xTl%d" % xsl)
                own = ti < 4
                for kind in ((0, 1) if own else (1,)):
                    dstT = QT if kind == 0 else KT
                    for hh in range(2):
                        bk = nbank()
                        for c in range(16):
                            P.add("pe", lambda e, c=c, hh=hh, kind=kind, bk=bk, xT=xT: e.matmul(
                                out=ps[bk][:], lhsT=wt[:, c, kind * 256 + hh * 128: kind * 256 + hh * 128 + 128],
                                rhs=xT[:, c, :], start=(c == 0), stop=(c == 15)),
                                reads=[xkf % c, "Wg%d_%d" % (wsl, kind)], writes=["ps%d" % bk])
                        key = ("Q%d_%d" if kind == 0 else "K%d_%d") % (hh, ti)
                        P.add("act", lambda e, hh=hh, bk=bk, dstT=dstT, ti=ti: e.activation(
                            out=dstT[:, hh, ti * 512:(ti + 1) * 512], in_=ps[bk][:], func=AF.Copy),
                            reads=["ps%d" % bk], writes=[key])
                for b in range(4):
                    bk = nbank()
                    for c in range(16):
                        P.add("pe", lambda e, c=c, b=b, bk=bk, xT=xT: e.matmul(
                            out=ps[bk][:, 0:256], lhsT=xT[:, c, b * 128:(b + 1) * 128],
                            rhs=wt[:, c, 512:768], start=(c == 0), stop=(c == 15)),
                            reads=[xkf % c, "Wg%d_2" % wsl], writes=["ps%d" % bk])
                    P.add("act", lambda e, b=b, bk=bk, ti=ti: e.activation(
                        out=Vt[:, ti * 4 + b, :, 0:128],
                        in_=ps[bk][:, 0:256].rearrange("p (a b) -> p a b", a=2), func=AF.Copy),
                        reads=["ps%d" % bk, "Vones"], writes=["V%d" % (ti * 4 + b)])

            if debug == "inproj":
                return early(["Q0_%d" % t for t in range(4)] + ["Q1_%d" % t for t in range(4)],
                             QT.rearrange("p a b -> p (a b)"))
            SKIP_T = 125.0
            steps = []
            for hh in range(2):
                h = 2 * g + hh
                m = SLOPES[h]
                for t in range(4):
                    cats = []
                    far = [kb for kb in range(16, 32) if m * (128 * (kb - 16) + 1) <= SKIP_T]
                    cats.append(("far", far))
                    if t > 0:
                        lo = [kb for kb in range(0, 4 * t) if m * (128 * (4 * t - kb - 1) + 1) <= SKIP_T]
                        cats.append(("lo", lo))
                    cats.append(("diag", list(range(4 * t, 4 * t + 4))))
                    if t < 3:
                        hi = [kb for kb in range(4 * t + 4, 16) if m * (128 * (kb - 4 * t - 4) + 1) <= SKIP_T]
                        cats.append(("hi", hi))
                    cats = [(cn, kbs) for cn, kbs in cats if kbs]
                    nsteps = sum(len(kbs) for _, kbs in cats)
                    cnt = 0
                    for ci, (cname, kbs) in enumerate(cats):
                        for ki, kb in enumerate(kbs):
                            cnt += 1
                            steps.append(dict(hh=hh, h=h, m=m, t=t, cname=cname, kb=kb, ki=ki,
                                              last=(ki == len(kbs) - 1), first_cat=(ci == 0),
                                              final=(cnt == nsteps)))

            def emit_qk(sp_):
                esl = rot["E"] % 2
                rot["E"] += 1
                sp_["esl"] = esl
                hh, kb, t = sp_["hh"], sp_["kb"], sp_["t"]
                for c in range(2):
                    bk = c * 2 + esl
                    P.add("pe", lambda e, c=c, kb=kb, bk=bk, hh=hh, t=t: e.matmul(
                        out=ps[bk][:], lhsT=KT[c * 64:(c + 1) * 64, hh, kb * 128:(kb + 1) * 128],
                        rhs=QT[c * 64:(c + 1) * 64, hh, t * 512:(t + 1) * 512], start=True, stop=True),
                        reads=["K%d_%d" % (hh, kb // 4), "Q%d_%d" % (hh, t)], writes=["ps%d" % bk])

            def emit_exp(sp_):
                esl, hh, h, m, kb, t, cname = (sp_[k] for k in ("esl", "hh", "h", "m", "kb", "t", "cname"))
                for c in range(2):
                    bk = c * 2 + esl
                    if cname == "diag":
                        a = kb - 4 * t
                        w0 = 384 - 128 * a
                        P.add("dve", lambda e, bk=bk, w0=w0, c=c, m=m: e.scalar_tensor_tensor(
                            out=sdiag[c], in0=tabs[:, T_TW + w0:T_TW + w0 + 512], scalar=-m / SCALE,
                            in1=ps[bk][:], op0=ALU.mult, op1=ALU.add),
                            reads=["ps%d" % bk, "tabs"], writes=["sdiag%d" % c])
                        sp_["exp_op"] = P.add("act", lambda e, c=c, esl=esl: e.activation(
                            out=Et[c][esl], in_=sdiag[c], func=AF.Exp, scale=SCALE),
                            reads=["sdiag%d" % c], writes=["E%d%d" % (c, esl)])
                    else:
                        if cname == "far":
                            col = T_KBFAR + (kb - 16) * 8 + h
                        elif cname == "lo":
                            col = T_KBLO + (4 * t - kb - 1) * 8 + h
                        else:
                            col = T_KBHI + (kb - 4 * t - 4) * 8 + h
                        sp_["exp_op"] = P.add("act", lambda e, c=c, esl=esl, bk=bk, col=col: e.activation(
                            out=Et[c][esl], in_=ps[bk][:], func=AF.Exp,
                            bias=tabs[:, col:col + 1], scale=SCALE),
                            reads=["ps%d" % bk, "tabs"], writes=["E%d%d" % (c, esl)])

            def emit_pv(sp_):
                esl, hh, kb, ki, last = (sp_[k] for k in ("esl", "hh", "kb", "ki", "last"))
                for c in range(2):
                    for qb in range(4):
                        a = c * 4 + qb
                        bk = 4 + a // 3
                        o0 = (a % 3) * 130
                        P.add("pe", lambda e, c=c, qb=qb, bk=bk, o0=o0, esl=esl, kb=kb, hh=hh, ki=ki, a=a, last=last:
                              e.matmul(out=ps[bk][:, o0:o0 + 129], lhsT=Et[c][esl][:, qb * 128:(qb + 1) * 128],
                                       rhs=Vt[:, kb, hh, 0:129], start=(ki == 0 and a % 3 == 0), stop=last,
                                       skip_group_check=True),
                              reads=["E%d%d" % (c, esl), "V%d" % kb], writes=["ps%d" % bk])

            def emit_evac(sp_):
                h, t, cname, first_cat = (sp_[k] for k in ("h", "t", "cname", "first_cat"))
                ob = (sp_["hh"] * 4 + t) % 2
                Oacc = Oacc2[ob]
                for c in range(2):
                    for qb in range(4):
                        a = c * 4 + qb
                        bk = 4 + a // 3
                        o0 = (a % 3) * 130
                        if cname == "far":
                            fac = qf[:, (4 * t + qb) * 8 + h:(4 * t + qb) * 8 + h + 1]
                        elif cname == "lo":
                            fac = qf[:, 128 + qb * 8 + h:128 + qb * 8 + h + 1]
                        elif cname == "hi":
                            fac = qf[:, 160 + qb * 8 + h:160 + qb * 8 + h + 1]
                        else:
                            fac = None
                        if first_cat and fac is not None:
                            P.add("dve", lambda e, a=a, bk=bk, o0=o0, fac=fac: e.tensor_scalar(
                                out=Oacc[:, a, 0:129], in0=ps[bk][:, o0:o0 + 129], scalar1=fac, scalar2=None,
                                op0=ALU.mult),
                                reads=["ps%d" % bk, "qf"], writes=["Oacc%d_%d" % (ob, a)])
                        elif first_cat:
                            P.add("dve", lambda e, a=a, bk=bk, o0=o0: e.tensor_copy(
                                out=Oacc[:, a, 0:129], in_=ps[bk][:, o0:o0 + 129]),
                                reads=["ps%d" % bk], writes=["Oacc%d_%d" % (ob, a)])
                        elif fac is None:
                            P.add("dve", lambda e, a=a, bk=bk, o0=o0: e.tensor_tensor(
                                out=Oacc[:, a, 0:129], in0=ps[bk][:, o0:o0 + 129], in1=Oacc[:, a, 0:129],
                                op=ALU.add),
                                reads=["ps%d" % bk, "Oacc%d_%d" % (ob, a)], writes=["Oacc%d_%d" % (ob, a)])
                        else:
                            P.add("dve", lambda e, a=a, bk=bk, o0=o0, fac=fac: e.scalar_tensor_tensor(
                                out=Oacc[:, a, 0:129], in0=ps[bk][:, o0:o0 + 129], scalar=fac,
                                in1=Oacc[:, a, 0:129], op0=ALU.mult, op1=ALU.add),
                                reads=["ps%d" % bk, "Oacc%d_%d" % (ob, a), "qf"], writes=["Oacc%d_%d" % (ob, a)])

            def emit_finalize(sp_):
                h, t = sp_["h"], sp_["t"]
                ob = (sp_["hh"] * 4 + t) % 2
                Oacc = Oacc2[ob]
                allO = ["Oacc%d_%d" % (ob, a) for a in range(8)]
                Ov = Oacc.rearrange("p (c q) n -> p c q n", c=2)
                th = []
                th.append(lambda: P.add("dve", lambda e: e.reciprocal(out=rz.rearrange("p q c -> p c q"), in_=Ov[:, :, :, 128]),
                                        reads=allO, writes=["rz"]))
                th.append(lambda: P.add("dve", lambda e: e.tensor_scalar(out=rz1l, in0=rz[:, :, 1], scalar1=neglam[:, 0:1],
                                                                        scalar2=None, op0=ALU.mult),
                                        reads=["rz", "neglam"], writes=["rz1l"]))
                th.append(lambda: P.add("dve", lambda e: e.tensor_tensor(out=tA, in0=Ov[:, 0, :, 0:128],
                                                                        in1=rz[:, :, 0:1].to_broadcast([128, 4, 128]), op=ALU.mult),
                                        reads=allO + ["rz"], writes=["tA"]))
                th.append(lambda: P.add("dve", lambda e: e.tensor_tensor(out=tB, in0=Ov[:, 1, :, 0:128],
                                                                        in1=rz1l.unsqueeze(2).to_broadcast([128, 4, 128]), op=ALU.mult),
                                        reads=allO + ["rz1l"], writes=["tB"]))
                th.append(lambda: P.add("dve", lambda e: e.tensor_tensor(out=tA, in0=tA, in1=tB, op=ALU.add),
                                        reads=["tA", "tB"], writes=["tA"]))
                th.append(lambda: P.add("dve", lambda e: e.tensor_tensor(out=tB, in0=tA, in1=tA, op=ALU.mult),
                                        reads=["tA"], writes=["tB"]))
                th.append(lambda: P.add("dve", lambda e: e.tensor_reduce(out=ssum, in_=tB, axis=AX.X, op=ALU.add),
                                        reads=["tB"], writes=["ssum"]))
                th.append(lambda: P.add("act", lambda e: e.activation(out=rstd, in_=ssum, func=AF.Sqrt, bias=LN_EPS,
                                                                     scale=1.0 / 128.0),
                                        reads=["ssum"], writes=["rstd0"]))
                th.append(lambda: P.add("dve", lambda e: e.reciprocal(out=rstd, in_=rstd), reads=["rstd0"], writes=["rstd"]))
                th.append(lambda: P.add("dve", lambda e: e.tensor_tensor(out=tB, in0=tA,
                                                                        in1=rstd.unsqueeze(2).to_broadcast([128, 4, 128]), op=ALU.mult),
                                        reads=["tA", "rstd"], writes=["tB"]))
                th.append(lambda: P.add("dve", lambda e: e.tensor_tensor(out=ofb, in0=tB,
                                                                        in1=gainT[:].unsqueeze(1).to_broadcast([128, 4, 128]), op=ALU.mult),
                                        reads=["tB", "gainT"], writes=["ofb"]))
                pv = psTb[7][:, 0:512]

                def tr():
                    for qb in range(4):
                        P.add("pe", lambda e, qb=qb: e.transpose(
                            out=pv[:, qb * 128:(qb + 1) * 128], in_=ofb[:, qb, :], identity=ident[:]),
                            reads=["ofb", "ident"], writes=["ps7"])
                th.append(tr)
                th.append(lambda: P.add("dve", lambda e: e.tensor_copy(out=attnT[:, h, t * 512:(t + 1) * 512], in_=pv),
                                        reads=["ps7"], writes=["attnT%d_%d" % (h, t)]))
                return th

            pending = []
            ncast_g = cast_plan[g]
            cast_stride = max(1, len(steps) // max(1, ncast_g)) if ncast_g else 1
            emit_qk(steps[0])
            for i, sp_ in enumerate(steps):
                if i + 1 < len(steps):
                    emit_qk(steps[i + 1])
                emit_exp(sp_)
                emit_pv(sp_)
                if sp_["last"]:
                    emit_evac(sp_)
                if ncast_g and i % cast_stride == 0 and i // cast_stride < ncast_g:
                    emit_cast(sp_["exp_op"])
                for _ in range(3):
                    if pending:
                        pending.pop(0)()
                if sp_["final"]:
                    pending.extend(emit_finalize(sp_))
            while pending:
                pending.pop(0)()

        while cast_jobs:
            emit_cast(None)

        if debug == "attn":
            allA = ["attnT%d_%d" % (h, t) for h in range(8) for t in range(4)]
            P.add("sp", lambda e: e.dma_start(out=dbg, in_=attnT[:].rearrange("p a b -> p (a b)")),
                  reads=allA, writes=["dbgout"], dma="o_dbg")
            P.add("sp", lambda e: e.dma_start(out=out_v[:, 0, 0:128], in_=gainT[:]), reads=["gainT", "dbgout"],
                  writes=["outx"], dma="o_fin")
            P.add("sp", lambda e: e.nop(), reads=["outx", "dbgout"], writes=["fin"])
            P.emit()
            return nc

        P.barrier()
        cv = Carver()
        xs2 = cv.take([4, 2048], BF16)
        xT2 = cv.take([16, 512], BF16)
        r1o = cv.off
        x32 = cv.take([4, 2048], F32)
        r1e = cv.off
        cv.off = r1o
        sv = cv.take([4, 1024], F32)
        vn = cv.take([4, 1024], BF16)
        uT = cv.take([8, 512], BF16)
        assert cv.off <= r1e
        cv.off = r1e
        r2o = cv.off
        hT = cv.take([32, 512], BF16)
        r2e = cv.off
        cv.off = r2o
        mergedT = cv.take([16, 512], BF16)
        _save = cv.off
        cv.off = r2o
        sgG = cv.take([1024], F32)
        sgB = cv.take([1024], F32)
        BsT = cv.take([8, 128], F32)
        assert cv.off <= _save
        cv.off = _save
        sgoT = cv.take([8, 512], BF16)
        g0t = cv.take([512], F32)
        g1t = cv.take([512], F32)
        gtmp = cv.take([512], F32)
        assert cv.off <= r2e
        cv.off = r2e
        NWB = 3
        wb = [cv.take([8192], BF16) for _ in range(NWB)]
        lnT = cv.take([2048], F32)
        WsT = cv.take([8, 128], BF16)
        bgT = cv.take([32], F32)
        rtmp = [cv.take([512], F32) for _ in range(2)]
        stats = cv.take([4, 6], F32)
        mv = cv.take([2], F32)
        rs = cv.take([1], F32)
        mvb = cv.take([4, 2], F32)
        rsb = cv.take([4], F32)

        R1A, R1B, R2A, R2B = "R1A", "R1B", "R2A", "R2B"
        SUB = ["a", "b", "c", "d"]

        P.add("sp", lambda e: e.dma_start(out=bgT, in_=b_gate), writes=["bgT"], dma="c_bgT")
        P.add("pool", lambda e: e.dma_start(out=WsT, in_=sg_w_s), writes=["WsT"], dma="c_WsT")

        st = {"wb": 0, "bank": 0, "rt": 0}

        def wslot():
            i = st["wb"] % NWB
            st["wb"] += 1
            return i

        def wkeys(i, n=4):
            return ["wb%d%s" % (i, SUB[k]) for k in range(n)]

        def bank6():
            b = st["bank"] % 6
            st["bank"] += 1
            return b

        def ln_stats_g(buf, b, nfree, keyin, keyout, extra_reads, gtab, gkey):
            nch = nfree // 512
            for ch in range(nch):
                P.add("dve", lambda e, ch=ch: e.bn_stats(out=stats[:, ch, :], in_=buf[:, b, ch * 512:(ch + 1) * 512]),
                      reads=[keyin] + extra_reads, writes=["stats%d" % ch])
            P.add("dve", lambda e: e.bn_aggr(out=mvb[:, b, :], in_=stats[:, 0:nch, :].rearrange("p a b -> p (a b)")),
                  reads=["stats%d" % ch for ch in range(nch)], writes=["mv%d" % b])
            P.add("act", lambda e: e.activation(out=rsb[:, b:b + 1], in_=mvb[:, b, 1:2], func=AF.Sqrt, bias=LN_EPS, scale=1.0),
                  reads=["mv%d" % b], writes=["rs0_%d" % b])
            P.add("dve", lambda e: e.reciprocal(out=rsb[:, b:b + 1], in_=rsb[:, b:b + 1]), reads=["rs0_%d" % b],
                  writes=["rs%d" % b])
            P.add("dve", lambda e: e.scalar_tensor_tensor(out=buf[:, b, 0:nfree], in0=buf[:, b, 0:nfree],
                                                          scalar=mvb[:, b, 0:1], in1=gtab, op0=ALU.subtract, op1=ALU.mult),
                  reads=[keyin, "mv%d" % b, gkey] + extra_reads, writes=[keyout])

        def ln_apply_b(outap, buf, b, nfree, keyin, keyout, extra_reads, btab, bkey):
            P.add("dve", lambda e: e.scalar_tensor_tensor(out=outap, in0=buf[:, b, 0:nfree], scalar=rsb[:, b:b + 1],
                                                          in1=btab, op0=ALU.mult, op1=ALU.add),
                  reads=[keyin, "rs%d" % b, bkey] + extra_reads, writes=[keyout])

        for t in range(4):
            P.add("sp", lambda e: e.dma_start(out=sgG, in_=sg_ln_g[0, :].partition_broadcast(128)),
                  reads=[R2A], writes=["sgG", R2B], dma="c_sgG")
            P.add("sp", lambda e: e.dma_start(out=sgB, in_=sg_ln_b[0, :].partition_broadcast(128)),
                  reads=[R2A], writes=["sgB"], dma="c_sgB")
            P.add("sp", lambda e: e.dma_start(out=BsT.rearrange("p a b -> p (a b)"),
                                              in_=sg_b_s[0, :].partition_broadcast(128)),
                  reads=[R2A], writes=["BsT"], dma="c_BsT")
            if t == 0:
                P.add("sp", lambda e, t=t: e.dma_start(out=xT2.rearrange("p a b -> p (a b)"), in_=xTs[t]),
                      reads=["xTs%d" % t], writes=["p2xT%d" % c for c in range(16)], dma="p2xT")
            xTk = ["p2xT%d" % c for c in range(16)]
            firstA = True
            for jj in range(2):
                wi = wslot()
                P.add("pool", lambda e, jj=jj, wi=wi: e.dma_start(
                    out=wb[wi].rearrange("p (c n) -> p c n", c=16),
                    in_=w_in_v[:, :, U_OFF + jj * 512:U_OFF + (jj + 1) * 512]),
                    writes=wkeys(wi), dma="wb%d" % wi)
                wv = wb[wi].rearrange("p (c n) -> p c n", c=16)
                for j4 in range(4):
                    j = jj * 4 + j4
                    bk = bank6()
                    for c in range(16):
                        P.add("pe", lambda e, c=c, j4=j4, bk=bk, wv=wv: e.matmul(
                            out=ps[bk][:], lhsT=wv[:, c, j4 * 128:(j4 + 1) * 128], rhs=xT2[:, c, :],
                            start=(c == 0), stop=(c == 15)),
                            reads=["p2xT%d" % c] + wkeys(wi), writes=["ps%d" % bk])
                    P.add("act", lambda e, j=j, bk=bk: e.activation(out=uT[:, j, :], in_=ps[bk][:], func=AF.Gelu),
                          reads=["ps%d" % bk, R1A], writes=["uT%d" % j] + ([R1B] if firstA else []))
                    firstA = False
            for jj in range(2):
                wi = wslot()
                P.add("pool", lambda e, jj=jj, wi=wi: e.dma_start(
                    out=wb[wi].rearrange("p (c n) -> p c n", c=16),
                    in_=w_in_v[:, :, SGV_OFF + jj * 512:SGV_OFF + (jj + 1) * 512]),
                    writes=wkeys(wi), dma="wb%d" % wi)
                wv = wb[wi].rearrange("p (c n) -> p c n", c=16)
                for b in range(4):
                    bk = bank6()
                    for c in range(16):
                        P.add("pe", lambda e, c=c, b=b, bk=bk, wv=wv: e.matmul(
                            out=ps[bk][:], lhsT=xT2[:, c, b * 128:(b + 1) * 128], rhs=wv[:, c, :],
                            start=(c == 0), stop=(c == 15)),
                            reads=["p2xT%d" % c] + wkeys(wi), writes=["ps%d" % bk])
                    P.add("act", lambda e, b=b, jj=jj, bk=bk: e.activation(
                        out=sv[:, b, jj * 512:(jj + 1) * 512], in_=ps[bk][:], func=AF.Gelu),
                        reads=["ps%d" % bk, R1A], writes=["sv%d_%d" % (b, jj)])
            for b in range(4):
                ln_stats_g(sv, b, 1024, "sv%d_0" % b, "svg%d" % b, [R1A, "sv%d_1" % b], sgG, "sgG")
                ln_apply_b(vn[:, b, :], sv, b, 1024, "svg%d" % b, "vn%d" % b, [R1A], sgB, "sgB")
            firstA2 = True
            for b in range(4):
                for gh in range(2):
                    bk = bank6()
                    for gi in range(4):
                        g_ = gh * 4 + gi
                        P.add("pe", lambda e, b=b, gi=gi, g_=g_, bk=bk: e.matmul(
                            out=ps[bk][:, gi * 128:(gi + 1) * 128], lhsT=vn[:, b, g_ * 128:(g_ + 1) * 128],
                            rhs=WsT[:, g_, :], start=True, stop=True),
                            reads=["vn%d" % b, "WsT", R1A], writes=["ps%d" % bk])
                    P.add("dve", lambda e, gh=gh, bk=bk: e.tensor_tensor(
                        out=gtmp.rearrange("p (a b) -> p a b", a=4), in0=ps[bk][:].rearrange("p (a b) -> p a b", a=4),
                        in1=BsT[:, gh * 4:gh * 4 + 4, :], op=ALU.add),
                        reads=["ps%d" % bk, "BsT", R2A], writes=["gtmp"] + ([R2B] if firstA2 else []))
                    firstA2 = False
                    P.add("dve", lambda e, gh=gh, b=b: e.tensor_tensor(
                        out=sgoT[:, gh * 4:gh * 4 + 4, b * 128:(b + 1) * 128],
                        in0=gtmp.rearrange("p (a b) -> p a b", a=4),
                        in1=uT[:, gh * 4:gh * 4 + 4, b * 128:(b + 1) * 128], op=ALU.mult),
                        reads=["gtmp", R1A, R2A] + ["uT%d" % (gh * 4 + k) for k in range(4)],
                        writes=["sgoT%d_%d" % (gh, b)])
            sgok = ["sgoT%d_%d" % (gh, b) for gh in range(2) for b in range(4)]
            for j in range(16):
                wi = wslot()
                w = wb[wi]
                wg0 = w[:, 0:2048].rearrange("p (c n) -> p c n", c=16)
                wg1 = w[:, 2048:4096].rearrange("p (c n) -> p c n", c=16)
                wa = w[:, 4096:5120].rearrange("p (c n) -> p c n", c=8)
                wsg = w[:, 5120:6144].rearrange("p (c n) -> p c n", c=8)
                wk = wkeys(wi)
                P.add("pool", lambda e, j=j, wg0=wg0: e.dma_start(
                    out=wg0, in_=w_in_v[:, :, GATE_OFF + j * 128:GATE_OFF + (j + 1) * 128]),
                    writes=[wk[0]], dma="wb%da" % wi)
                P.add("pool", lambda e, j=j, wg1=wg1: e.dma_start(
                    out=wg1, in_=w_in_v[:, :, GATE_OFF + D + j * 128:GATE_OFF + D + (j + 1) * 128]),
                    writes=[wk[1]], dma="wb%db" % wi)
                P.add("pool", lambda e, j=j, wa=wa: e.dma_start(out=wa, in_=w_ba_v[:, :, j * 128:(j + 1) * 128]),
                      writes=[wk[2]], dma="wb%dc" % wi)
                P.add("pool", lambda e, j=j, wsg=wsg: e.dma_start(out=wsg, in_=w_bs_v[:, :, j * 128:(j + 1) * 128]),
                      writes=[wk[3]], dma="wb%dd" % wi)
                bks = [bank6() for _ in range(4)]
                for gi, (wgx, gt) in enumerate(((wg0, g0t), (wg1, g1t))):
                    bk = bks[gi]
                    for c in range(16):
                        P.add("pe", lambda e, c=c, bk=bk, wgx=wgx: e.matmul(
                            out=ps[bk][:], lhsT=wgx[:, c, :], rhs=xT2[:, c, :], start=(c == 0), stop=(c == 15)),
                            reads=["p2xT%d" % c, wk[gi]], writes=["ps%d" % bk])
                    P.add("act", lambda e, bk=bk, gt=gt, gi=gi, j=j: e.activation(
                        out=gt, in_=ps[bk][:], func=AF.Sigmoid, bias=bgT[:, gi * 16 + j:gi * 16 + j + 1], scale=1.0),
                        reads=["ps%d" % bk, "bgT", R2A], writes=["g%dt" % gi])
                for c in range(8):
                    P.add("pe", lambda e, c=c, bk=bks[2], wa=wa, t=t: e.matmul(
                        out=ps[bk][:], lhsT=wa[:, c, :], rhs=attnT[:, c, t * 512:(t + 1) * 512],
                        start=(c == 0), stop=(c == 7)),
                        reads=[wk[2]], writes=["ps%d" % bks[2]])
                for c in range(8):
                    P.add("pe", lambda e, c=c, bk=bks[3], wsg=wsg: e.matmul(
                        out=ps[bk][:], lhsT=wsg[:, c, :], rhs=sgoT[:, c, :], start=(c == 0), stop=(c == 7)),
                        reads=[wk[3], R2A] + sgok, writes=["ps%d" % bks[3]])
                P.add("dve", lambda e, bk=bks[2]: e.tensor_tensor(out=g0t, in0=g0t, in1=ps[bk][:], op=ALU.mult),
                      reads=["g0t", "ps%d" % bks[2], R2A], writes=["g0m"])
                P.add("dve", lambda e, bk=bks[3]: e.tensor_tensor(out=g1t, in0=g1t, in1=ps[bk][:], op=ALU.mult),
                      reads=["g1t", "ps%d" % bks[3], R2A], writes=["g1m"])
                P.add("dve", lambda e, j=j: e.tensor_tensor(out=mergedT[:, j, :], in0=g0t, in1=g1t, op=ALU.add),
                      reads=["g0m", "g1m", R2A], writes=["mg%d" % j, "g0t", "g1t", "sgG", "sgB", "BsT"])
            mgk = ["mg%d" % j for j in range(16)]
            P.add("sp", lambda e, t=t: e.dma_start(out=x32, in_=x_v[:, 4 * t:4 * t + 4, :]),
                  reads=[R1B], writes=["r%d" % b for b in range(4)] + [R1A], dma="x32")
            for ct in range(4):
                wi = wslot()
                P.add("pool", lambda e, ct=ct, wi=wi: e.dma_start(
                    out=wb[wi].rearrange("p (c n) -> p c n", c=16), in_=w_o_v[:, :, ct * 512:(ct + 1) * 512]),
                    writes=wkeys(wi), dma="wb%d" % wi)
                wv = wb[wi].rearrange("p (c n) -> p c n", c=16)
                for b in range(4):
                    bk = bank6()
                    for c in range(16):
                        P.add("pe", lambda e, c=c, b=b, bk=bk, wv=wv: e.matmul(
                            out=ps[bk][:], lhsT=mergedT[:, c, b * 128:(b + 1) * 128], rhs=wv[:, c, :],
                            start=(c == 0), stop=(c == 15)),
                            reads=["mg%d" % c, R2A] + wkeys(wi), writes=["ps%d" % bk])
                    P.add("dve", lambda e, b=b, ct=ct, bk=bk: e.scalar_tensor_tensor(
                        out=x32[:, b, ct * 512:(ct + 1) * 512], in0=x32[:, b, ct * 512:(ct + 1) * 512], scalar=ALPHA,
                        in1=ps[bk][:], op0=ALU.mult, op1=ALU.add),
                        reads=["ps%d" % bk, "r%d" % b, R1B], writes=["r%d" % b])
            for (gsrc, bsrc, stage) in ((ln1_g, ln1_b, 1),):
                P.add("sp", lambda e, gsrc=gsrc: e.dma_start(out=lnT, in_=gsrc[0, :].partition_broadcast(128)),
                      writes=["lnT"], dma="lnT")
                for b in range(4):
                    ln_stats_g(x32, b, 2048, "r%d" % b, "rg%d" % b, [R1B], lnT, "lnT")
                P.add("sp", lambda e, bsrc=bsrc: e.dma_start(out=lnT, in_=bsrc[0, :].partition_broadcast(128)),
                      writes=["lnT"], dma="lnT")
                for b in range(4):
                    ln_apply_b(x32[:, b, :], x32, b, 2048, "rg%d" % b, "x1_%d" % b, [R1B], lnT, "lnT")
                    P.add("act", lambda e, b=b: e.activation(out=xs2[:, b, :], in_=x32[:, b, :], func=AF.Copy),
                          reads=["x1_%d" % b, R1B], writes=["p2xs"])
            transpose_tile(xs2, "p2xs", xT2, "p2xT%d")
            for fh in range(2):
                firstB2 = True
                for f4 in range(8):
                    wi = wslot()
                    c0 = (fh * 32 + f4 * 4) * 128
                    kup = fh * 8 + f4
                    P.add("sp", lambda e, kup=kup, wi=wi: e.dma_start(out=wb[wi], in_=scr_up[kup]),
                          reads=["scr_up%d" % kup], writes=wkeys(wi), dma="wbs%d" % wi)
                    wv = wb[wi].rearrange("p (c n) -> p c n", c=16)
                    for fi in range(4):
                        bk = bank6()
                        for c in range(16):
                            P.add("pe", lambda e, c=c, fi=fi, bk=bk, wv=wv: e.matmul(
                                out=ps[bk][:], lhsT=wv[:, c, fi * 128:(fi + 1) * 128], rhs=xT2[:, c, :],
                                start=(c == 0), stop=(c == 15)),
                                reads=["p2xT%d" % c] + wkeys(wi), writes=["ps%d" % bk])
                        ri = st["rt"] % 2
                        st["rt"] += 1
                        P.add("act", lambda e, bk=bk, ri=ri: e.activation(out=rtmp[ri], in_=ps[bk][:], func=AF.Relu),
                              reads=["ps%d" % bk], writes=["rtmp%d" % ri])
                        kk = f4 * 4 + fi
                        P.add("dve", lambda e, ri=ri, kk=kk: e.tensor_tensor(out=hT[:, kk, :], in0=rtmp[ri], in1=rtmp[ri],
                                                                          op=ALU.mult),
                              reads=["rtmp%d" % ri, R2B], writes=["hT%d" % kk] + ([R2A] if firstB2 else []))
                        firstB2 = False
                if fh == 1 and t < 3:
                    P.add("pool", lambda e, t=t: e.dma_start(out=xT2.rearrange("p a b -> p (a b)"), in_=xTs[t + 1]),
                          reads=["xTs%d" % (t + 1)], writes=["p2xT%d" % c for c in range(16)], dma="p2xTp")
                for ct in range(4):
                    bset = [0, 1, 2, 3] if ct % 2 == 0 else [4, 5, 6, 7]
                    for fp in range(2):
                        wi = wslot()
                        r0 = fh * 32 + fp * 16
                        kdn = (fh * 2 + fp) * 4 + ct
                        P.add("sp", lambda e, kdn=kdn, wi=wi: e.dma_start(out=wb[wi], in_=scr_dn[kdn]),
                              reads=["scr_dn%d" % kdn], writes=wkeys(wi), dma="wbs%d" % wi)
                        wv = wb[wi].rearrange("p (c n) -> p c n", c=16)
                        for b in range(4):
                            bk = bset[b]
                            for k in range(16):
                                P.add("pe", lambda e, k=k, b=b, bk=bk, wv=wv, fp=fp: e.matmul(
                                    out=ps[bk][:], lhsT=hT[:, fp * 16 + k, b * 128:(b + 1) * 128], rhs=wv[:, k, :],
                                    start=(fp == 0 and k == 0), stop=(fp == 1 and k == 15)),
                                    reads=["hT%d" % (fp * 16 + k), R2B] + wkeys(wi), writes=["ps%d" % bk])
                    for b in range(4):
                        bk = bset[b]
                        if fh == 0:
                            P.add("dve", lambda e, b=b, ct=ct, bk=bk: e.scalar_tensor_tensor(
                                out=x32[:, b, ct * 512:(ct + 1) * 512], in0=x32[:, b, ct * 512:(ct + 1) * 512],
                                scalar=ALPHA, in1=ps[bk][:], op0=ALU.mult, op1=ALU.add),
                                reads=["ps%d" % bk, "x1_%d" % b, R1B], writes=["x1_%d" % b])
                        else:
                            P.add("dve", lambda e, b=b, ct=ct, bk=bk: e.tensor_tensor(
                                out=x32[:, b, ct * 512:(ct + 1) * 512], in0=x32[:, b, ct * 512:(ct + 1) * 512],
                                in1=ps[bk][:], op=ALU.add),
                                reads=["ps%d" % bk, "x1_%d" % b, R1B], writes=["x1_%d" % b])
            P.add("sp", lambda e: e.dma_start(out=lnT, in_=ln2_g[0, :].partition_broadcast(128)),
                  writes=["lnT"], dma="lnT")
            for b in range(4):
                ln_stats_g(x32, b, 2048, "x1_%d" % b, "y2g%d" % b, [R1B], lnT, "lnT")
            P.add("sp", lambda e: e.dma_start(out=lnT, in_=ln2_b[0, :].partition_broadcast(128)),
                  writes=["lnT"], dma="lnT")
            for b in range(4):
                ln_apply_b(x32[:, b, :], x32, b, 2048, "y2g%d" % b, "r%d" % b, [R1B], lnT, "lnT")
                P.add("sp", lambda e, b=b, t=t: e.dma_start(out=out_v[:, 4 * t + b, :], in_=x32[:, b, :]),
                      reads=["r%d" % b, R1B], writes=["out%d" % (4 * t + b)], dma="o_out")
        P.add("sp", lambda e: e.nop(), reads=["out%d" % i for i in range(16)], writes=["fin"])

        P.emit()
    return nc


def make_tabs(half):
    p = np.arange(128, dtype=np.float64)[:, None]
    tabs = np.zeros((128, T_END), dtype=np.float64)
    m = np.array(SLOPES, dtype=np.float64)[None, :]
    for j in range(16):
        if half == 0:
            dk = (2048 + 128 * j + p) - 2047
        else:
            dk = 2048 - (128 * (15 - j) + p)
        tabs[:, T_KBFAR + j * 8:T_KBFAR + j * 8 + 8] = -m * dk
        if half == 0:
            dq = 2047 - (128 * j + p)
        else:
            dq = (128 * j + p)
        tabs[:, T_QFAR + j * 8:T_QFAR + j * 8 + 8] = -m * dq
    for Dd in range(1, 13):
        tabs[:, T_KBLO + (Dd - 1) * 8:T_KBLO + Dd * 8] = -m * (128 * Dd - p)
    for Dd in range(12):
        tabs[:, T_KBHI + Dd * 8:T_KBHI + Dd * 8 + 8] = -m * (128 * Dd + p + 1)
    for qb in range(4):
        tabs[:, T_QLO + qb * 8:T_QLO + qb * 8 + 8] = -m * (128 * qb + p)
        tabs[:, T_QHI + qb * 8:T_QHI + qb * 8 + 8] = -m * (511 - 128 * qb - p)
    xx = np.arange(896, dtype=np.float64)[None, :]
    tabs[:, T_TW:T_END] = np.abs(xx - p - 384)
    return tabs.astype(np.float32)


def make_in_maps(inputs):
    f = lambda a: np.ascontiguousarray(np.asarray(a, dtype=np.float32))
    x = f(inputs["x"])
    common = {
        "w_in": f(inputs["w_in"])[0],
        "lam4": np.stack([f(inputs["lambda_q1"])[0], f(inputs["lambda_k1"])[0],
                          f(inputs["lambda_q2"])[0], f(inputs["lambda_k2"])[0]]),
        "subln_g": f(inputs["attn_subln_g"]).reshape(1, 128),
        "sg_ln_g": f(inputs["sg_ln_g"]).reshape(1, 1024),
        "sg_ln_b": f(inputs["sg_ln_b"]).reshape(1, 1024),
        "sg_w_s": np.ascontiguousarray(f(inputs["sg_w_s"])[0].transpose(2, 0, 1)),
        "sg_b_s": f(inputs["sg_b_s"]).reshape(1, 1024),
        "b_gate": np.ascontiguousarray(f(inputs["b_gate"]).reshape(32, 128).T),
        "w_ba": f(inputs["w_branch_attn"])[0],
        "w_bs": f(inputs["w_branch_sg"])[0],
        "w_o": f(inputs["w_o"])[0],
        "ln1_g": f(inputs["ln1_g"]).reshape(1, D),
        "ln1_b": f(inputs["ln1_b"]).reshape(1, D),
        "w_up": f(inputs["w_up"])[0],
        "w_down": f(inputs["w_down"])[0],
        "ln2_g": f(inputs["ln2_g"]).reshape(1, D),
        "ln2_b": f(inputs["ln2_b"]).reshape(1, D),
        "ident": np.eye(128, dtype=np.float32),
    }
    maps = []
    for c in range(8):
        b, half = c // 2, c % 2
        xo = x[b, half * TOWN:(half + 1) * TOWN]
        xt = x[b, (1 - half) * TOWN:(2 - half) * TOWN]
        mp = dict(common)
        if half == 1:
            xt = xt.reshape(16, 128, D)[::-1].reshape(TOWN, D)
        mp["x_perm"] = np.ascontiguousarray(np.concatenate([xo, xt], axis=0))
        mp["tabs"] = make_tabs(half)
        maps.append(mp)
    return maps


def kernel(**inputs):
    nc = build_nc()
    maps = make_in_maps(inputs)
    res = run_bass_kernel_spmd(nc, maps, core_ids=list(range(8)))
    outp = np.zeros((4, SEQ, D), dtype=np.float32)
    for c in range(8):
        b, half = c // 2, c % 2
        outp[b, half * TOWN:(half + 1) * TOWN] = res.results[c]["out"]
    return outp
```
